# Optimizing a Trainium2 kernel written in Bass

```python
import math
import jax, jax.numpy as jnp
from jax import lax
import numpy as np

D_MODEL = 2048
BATCH = 4
SEQ = 8192
DEPTH = 2

MEM_LEN = 256
MEM_HEADS = 4
MEM_HEAD_DIM = D_MODEL // 8
MEM_WIDTH = MEM_HEADS * MEM_HEAD_DIM
SSD_INNER = 3 * D_MODEL // 2
SSD_HEAD_DIM = 64
SSD_HEADS = SSD_INNER // SSD_HEAD_DIM
SSD_GROUPS = 8
SSD_HEADS_PER_GROUP = SSD_HEADS // SSD_GROUPS
SSD_STATE = 128
SSD_CONV = 5
SSD_CHUNK = 128
SSD_NORM_GROUPS = 8
SSD_CONV_CH = SSD_INNER + 2 * SSD_GROUPS * SSD_STATE
SSD_IN = SSD_INNER + SSD_CONV_CH + 2 * SSD_HEADS + MEM_WIDTH
DIL_CONFIGS = ((128, 1), (512, 4), (2048, 16))
DIL_GROUPS = len(DIL_CONFIGS)
DIL_HEADS = 8
DIL_HEAD_DIM = D_MODEL // 16
DIL_WIDTH = DIL_HEADS * DIL_HEAD_DIM
DIL_IN = 3 * DIL_GROUPS * DIL_WIDTH + MEM_WIDTH
REL_BUCKETS = 32
REL_MAX_DISTANCE = 1024
D_FF = 11 * D_MODEL // 4
EPS = 1e-6

kernel_name = 'hybrid_ssd_dilated_memory_encoder'


def rmsnorm(x, g):
    xf = x.astype(jnp.float32)
    y = xf * lax.rsqrt(jnp.mean(xf * xf, axis=-1, keepdims=True) + EPS)
    return (y * g.astype(jnp.float32)).astype(x.dtype)


def swiglu(x, w_in, w_out):
    gate, up = jnp.split(x @ w_in, 2, axis=-1)
    return (jax.nn.silu(gate) * up) @ w_out


def t5_bucket(rel):
    half = REL_BUCKETS // 2
    exact = half // 2
    n = jnp.abs(rel)
    far = exact + (jnp.log(jnp.maximum(n, 1).astype(jnp.float32) / exact)
                   / math.log(REL_MAX_DISTANCE / exact) * (half - exact)).astype(jnp.int32)
    far = jnp.minimum(far, half - 1)
    return jnp.where(rel > 0, half, 0) + jnp.where(n < exact, n, far)


def memory_attention(q, memn, w_kv, q_gain, k_gain):
    b, S = q.shape[:2]
    kv = (memn @ w_kv).reshape(b, memn.shape[1], 2, MEM_HEADS, MEM_HEAD_DIM)
    k = rmsnorm(kv[:, :, 0], k_gain)
    v = kv[:, :, 1]
    q = rmsnorm(q, q_gain)
    s = jnp.einsum('bshd,bmhd->bhsm', q, k, preferred_element_type=jnp.float32) * MEM_HEAD_DIM ** -0.5
    p = jax.nn.softmax(s, axis=-1)
    o = jnp.einsum('bhsm,bmhd->bshd', p.astype(v.dtype), v)
    return o.reshape(b, S, MEM_WIDTH)


def ssd_chunked(xh, dt, A, Bm, Cm):
    b, S, G, R, P = xh.shape
    N = Bm.shape[-1]
    Q = SSD_CHUNK
    c = S // Q
    f32 = jnp.float32
    X = (xh.astype(f32) * dt[..., None]).reshape(b, c, Q, G, R, P)
    a = (dt * A).reshape(b, c, Q, G, R).transpose(0, 1, 3, 4, 2)
    a_cs = lax.cumsum(a, axis=4)
    Bc = Bm.astype(f32).reshape(b, c, Q, G, N)
    Cc = Cm.astype(f32).reshape(b, c, Q, G, N)
    tril = jnp.tril(jnp.ones((Q, Q), dtype=bool))
    Lmat = jnp.exp(jnp.where(tril, a_cs[..., :, None] - a_cs[..., None, :], -jnp.inf))
    CB = jnp.einsum('bclgn,bcsgn->bcgls', Cc, Bc)
    y_diag = jnp.einsum('bcgrls,bcsgrp->bclgrp', CB[:, :, :, None] * Lmat, X)
    decay_in = jnp.exp(a_cs[..., -1:] - a_cs).transpose(0, 1, 4, 2, 3)
    states = jnp.einsum('bclgn,bclgrp->bcgrpn', Bc, X * decay_in[..., None])
    chunk_decay = jnp.exp(a_cs[..., -1])

    def step(h, inp):
        s_c, d_c = inp
        return d_c[..., None, None] * h + s_c, h

    _, prev = lax.scan(step, jnp.zeros((b, G, R, P, N), f32),
                       (jnp.moveaxis(states, 1, 0), jnp.moveaxis(chunk_decay, 1, 0)))
    prev = jnp.moveaxis(prev, 0, 1)
    decay_out = jnp.exp(a_cs).transpose(0, 1, 4, 2, 3)
    y_off = jnp.einsum('bclgn,bcgrpn->bclgrp', Cc, prev) * decay_out[..., None]
    return (y_diag + y_off).reshape(b, S, G, R, P).astype(xh.dtype)


def ssd_layer(h, memn, w_in, conv_w, conv_b, dt_bias, A_log, d_skip, norm_g, w_out,
              mem_w_kv, mem_q_gain, mem_k_gain):
    b, S, _ = h.shape
    G, R, P, N = SSD_GROUPS, SSD_HEADS_PER_GROUP, SSD_HEAD_DIM, SSD_STATE
    proj = h @ w_in
    z = proj[..., :SSD_INNER]
    xbc = proj[..., SSD_INNER:SSD_INNER + SSD_CONV_CH]
    dt_raw = proj[..., SSD_INNER + SSD_CONV_CH:SSD_INNER + SSD_CONV_CH + 2 * SSD_HEADS]
    q_mem = proj[..., SSD_IN - MEM_WIDTH:].reshape(b, S, MEM_HEADS, MEM_HEAD_DIM)
    pad = SSD_CONV // 2
    xbc = lax.conv_general_dilated(xbc, conv_w[:, None, :], window_strides=(1,),
                                   padding=[(pad, pad)], dimension_numbers=('NWC', 'WIO', 'NWC'),
                                   feature_group_count=SSD_CONV_CH)
    xbc = jax.nn.silu(xbc + conv_b)
    xs = xbc[..., :SSD_INNER].reshape(b, S, G, R, P)
    Bm = xbc[..., SSD_INNER:SSD_INNER + G * N].reshape(b, S, G, N)
    Cm = xbc[..., SSD_INNER + G * N:].reshape(b, S, G, N)
    dt = jax.nn.softplus((dt_raw.reshape(b, S, 2, SSD_HEADS) + dt_bias).astype(jnp.float32))
    dt = dt.reshape(b, S, 2, G, R)
    A = -jnp.exp(A_log.astype(jnp.float32)).reshape(2, G, R)
    flip = lambda t: jnp.flip(t, axis=1)
    y_f = ssd_chunked(xs, dt[:, :, 0], A[0], Bm, Cm)
    y_b = flip(ssd_chunked(flip(xs), flip(dt[:, :, 1]), A[1], flip(Bm), flip(Cm)))
    y = y_f + y_b + d_skip.reshape(G, R)[..., None] * xs
    y = y.reshape(b, S, SSD_INNER) * jax.nn.silu(z)
    y = rmsnorm(y.reshape(b, S, SSD_NORM_GROUPS, SSD_INNER // SSD_NORM_GROUPS),
                norm_g.reshape(SSD_NORM_GROUPS, -1)).reshape(b, S, SSD_INNER)
    o_mem = memory_attention(q_mem, memn, mem_w_kv, mem_q_gain, mem_k_gain)
    return jnp.concatenate([y, o_mem], axis=-1) @ w_out


def dilated_branch(q, k, v, table, window, dilation):
    b, S, h, d = q.shape
    r = dilation
    K = window // (2 * dilation)
    W = K
    L = S // r
    nb = -(-L // W)
    Lp = nb * W

    def sub(t, lo, hi):
        t = t.reshape(b, L, r, h, d)
        return jnp.pad(t, ((0, 0), (lo, hi), (0, 0), (0, 0), (0, 0)))

    def band(t):
        t = sub(t, W, Lp - L + W).reshape(b, nb + 2, W, r, h, d)
        return jnp.concatenate([t[:, :-2], t[:, 1:-1], t[:, 2:]], axis=2)

    qs = sub(q, 0, Lp - L).reshape(b, nb, W, r, h, d)
    kb, vb = band(k), band(v)
    s = jnp.einsum('bnqrhd,bnkrhd->brhnqk', qs, kb, preferred_element_type=jnp.float32) * d ** -0.5
    rel = jnp.arange(3 * W)[None, :] - W - jnp.arange(W)[:, None]
    kpos = (jnp.arange(nb)[:, None] - 1) * W + jnp.arange(3 * W)[None, :]
    mask = (jnp.abs(rel) <= K)[None] & ((kpos >= 0) & (kpos < L))[:, None, :]
    bias = jnp.transpose(table[t5_bucket(rel * r)], (2, 0, 1)).astype(jnp.float32)
    s = jnp.where(mask, s + bias[:, None], -jnp.inf)
    lse = jax.nn.logsumexp(s, axis=-1)
    p = jnp.exp(s - lse[..., None])
    o = jnp.einsum('brhnqk,bnkrhd->bnqrhd', p.astype(vb.dtype), vb)
    o = o.reshape(b, Lp, r, h, d)[:, :L].reshape(b, S, h, d)
    lse = jnp.transpose(lse, (0, 3, 4, 1, 2)).reshape(b, Lp, r, h)[:, :L].reshape(b, S, h)
    return o, lse


def dilated_layer(h, memn, w_in, q_gain, k_gain, w_out, rel_bias, mem_w_kv, mem_q_gain, mem_k_gain):
    b, S, _ = h.shape
    proj = h @ w_in
    qkv = proj[..., :3 * DIL_GROUPS * DIL_WIDTH].reshape(b, S, DIL_GROUPS, 3, DIL_HEADS, DIL_HEAD_DIM)
    q_mem = proj[..., 3 * DIL_GROUPS * DIL_WIDTH:].reshape(b, S, MEM_HEADS, MEM_HEAD_DIM)
    outs, lses = [], []
    for g, (window, dilation) in enumerate(DIL_CONFIGS):
        q = rmsnorm(qkv[:, :, g, 0], q_gain[g])
        k = rmsnorm(qkv[:, :, g, 1], k_gain[g])
        o, l = dilated_branch(q, k, qkv[:, :, g, 2], rel_bias[:, g * DIL_HEADS:(g + 1) * DIL_HEADS],
                              window, dilation)
        outs.append(o)
        lses.append(l)
    wts = jax.nn.softmax(jnp.stack(lses), axis=0)
    o = jnp.sum(wts[..., None].astype(outs[0].dtype) * jnp.stack(outs), axis=0).reshape(b, S, DIL_WIDTH)
    o_mem = memory_attention(q_mem, memn, mem_w_kv, mem_q_gain, mem_k_gain)
    return jnp.concatenate([o, o_mem], axis=-1) @ w_out


def setup_inputs(seed: int = 0) -> dict:
    key = jax.random.key(seed)
    ks = jax.random.split(key, 26)
    f32 = jnp.float32
    na = (DEPTH + 1) // 2
    nbl = DEPTH // 2
    nrm = lambda k, shape, fan_in: jax.random.normal(k, shape, f32) * fan_in ** -0.5
    gain = lambda k, shape: 1.0 + 0.02 * jax.random.normal(k, shape, f32)
    dt0 = jnp.exp(jax.random.uniform(ks[17], (na, 2, SSD_HEADS), f32)
                  * (math.log(0.1) - math.log(0.001)) + math.log(0.001))
    return {
        'x': jax.random.normal(ks[0], (BATCH, SEQ, D_MODEL), f32),
        'mem': jax.random.normal(ks[1], (BATCH, MEM_LEN, D_MODEL), f32),
        'rel_bias': 0.5 * jax.random.normal(ks[2], (REL_BUCKETS, DIL_GROUPS * DIL_HEADS), f32),
        'ffn_norm': gain(ks[3], (DEPTH, 2, D_MODEL)),
        'ffn_w_in': nrm(ks[4], (DEPTH, 2, D_MODEL, 2 * D_FF), D_MODEL),
        'ffn_w_out': nrm(ks[5], (DEPTH, 2, D_FF, D_MODEL), D_FF),
        'mix_norm': gain(ks[6], (DEPTH, D_MODEL)),
        'mem_norm': gain(ks[7], (DEPTH, D_MODEL)),
        'mem_w_kv': nrm(ks[8], (DEPTH, D_MODEL, 2 * MEM_WIDTH), D_MODEL),
        'mem_q_gain': gain(ks[9], (DEPTH, MEM_HEAD_DIM)),
        'mem_k_gain': gain(ks[10], (DEPTH, MEM_HEAD_DIM)),
        'ssd_w_in': nrm(ks[11], (na, D_MODEL, SSD_IN), D_MODEL),
        'ssd_conv_w': nrm(ks[12], (na, SSD_CONV, SSD_CONV_CH), SSD_CONV),
        'ssd_conv_b': 0.02 * jax.random.normal(ks[13], (na, SSD_CONV_CH), f32),
        'ssd_dt_bias': dt0 + jnp.log(-jnp.expm1(-dt0)),
        'ssd_A_log': jnp.log(jax.random.uniform(ks[14], (na, 2, SSD_HEADS), f32, 1.0, 16.0)),
        'ssd_D': gain(ks[15], (na, SSD_HEADS)),
        'ssd_norm': gain(ks[16], (na, SSD_INNER)),
        'ssd_w_out': nrm(ks[18], (na, SSD_INNER + MEM_WIDTH, D_MODEL), SSD_INNER + MEM_WIDTH),
        'dil_w_in': nrm(ks[19], (nbl, D_MODEL, DIL_IN), D_MODEL),
        'dil_q_gain': gain(ks[20], (nbl, DIL_GROUPS, DIL_HEAD_DIM)),
        'dil_k_gain': gain(ks[21], (nbl, DIL_GROUPS, DIL_HEAD_DIM)),
        'dil_w_out': nrm(ks[22], (nbl, DIL_WIDTH + MEM_WIDTH, D_MODEL), DIL_WIDTH + MEM_WIDTH),
    }


def reference(x, mem, rel_bias, ffn_norm, ffn_w_in, ffn_w_out, mix_norm, mem_norm, mem_w_kv,
              mem_q_gain, mem_k_gain, ssd_w_in, ssd_conv_w, ssd_conv_b, ssd_dt_bias, ssd_A_log,
              ssd_D, ssd_norm, ssd_w_out, dil_w_in, dil_q_gain, dil_k_gain, dil_w_out):
    for i in range(DEPTH):
        x = x + 0.5 * swiglu(rmsnorm(x, ffn_norm[i, 0]), ffn_w_in[i, 0], ffn_w_out[i, 0])
        h = rmsnorm(x, mix_norm[i])
        memn = rmsnorm(mem, mem_norm[i])
        j = i // 2
        if i % 2 == 0:
            y = ssd_layer(h, memn, ssd_w_in[j], ssd_conv_w[j], ssd_conv_b[j], ssd_dt_bias[j],
                          ssd_A_log[j], ssd_D[j], ssd_norm[j], ssd_w_out[j],
                          mem_w_kv[i], mem_q_gain[i], mem_k_gain[i])
        else:
            y = dilated_layer(h, memn, dil_w_in[j], dil_q_gain[j], dil_k_gain[j], dil_w_out[j],
                              rel_bias, mem_w_kv[i], mem_q_gain[i], mem_k_gain[i])
        x = x + y
        x = x + 0.5 * swiglu(rmsnorm(x, ffn_norm[i, 1]), ffn_w_in[i, 1], ffn_w_out[i, 1])
    return x
```

```python
import numpy as np
import concourse.bass as bass
import concourse.mybir as mybir
from concourse.bass_utils import run_bass_kernel_spmd

F32 = mybir.dt.float32
BF16 = mybir.dt.bfloat16
AF = mybir.ActivationFunctionType
ALU = mybir.AluOpType
AX = mybir.AxisListType

D = 2048
KC = D // 128
DFF = 5632
JC = DFF // 128
EPS = 1e-6
SAME_ENGINE_SYNC = True


class Buf:
    __slots__ = ("name", "w", "r")

    def __init__(self, name):
        self.name = name
        self.w = []
        self.r = []


class Op:
    __slots__ = ("eng", "fn", "deps", "stream", "is_dma", "signaled", "count")

    def __init__(self, eng, fn, deps, stream=None):
        self.eng = eng
        self.fn = fn
        self.deps = deps
        self.stream = stream
        self.is_dma = stream is not None
        self.signaled = False
        self.count = 0


class Prog:
    ENGS = ("pe", "dve", "act", "pool", "sp")

    def __init__(self, nc):
        self.nc = nc
        self.ops = []
        self.bar = set()
        self.last = {}
        self.e = {"pe": nc.tensor, "dve": nc.vector, "act": nc.scalar, "pool": nc.gpsimd, "sp": nc.sync}

    def op(self, eng, fn, reads=(), writes=(), stream=None, accum=False):
        idx = len(self.ops)
        deps = set()
        for b in reads:
            deps.update(b.w)
        for b in writes:
            deps.update(b.w)
            deps.update(b.r)
        deps.discard(idx)
        deps.update(self.bar)
        red = {}
        for d in deps:
            p = self.ops[d]
            key = ("dma", p.stream) if p.is_dma else ("eng", p.eng)
            if d > red.get(key, -1):
                red[key] = d
        deps = set(red.values())
        self.ops.append(Op(eng, fn, deps, stream))
        self.last[("dma", stream) if stream is not None else ("eng", eng)] = idx
        for b in reads:
            b.r.append(idx)
        for b in writes:
            if accum:
                b.w.append(idx)
            else:
                b.w = [idx]
                b.r = []
        return idx

    def barrier(self):
        self.bar = set(self.last.values())

    def emit(self, final_wait_streams=()):
        nc = self.nc
        ops = self.ops
        for i, o in enumerate(ops):
            if o.is_dma:
                o.signaled = True
                continue
        for i, o in enumerate(ops):
            for d in o.deps:
                p = ops[d]
                if p.is_dma:
                    continue
                if p.eng != o.eng or (SAME_ENGINE_SYNC and o.eng != "pe") or o.is_dma:
                    p.signaled = True
        cnt = {}
        sems = {}
        for o in ops:
            key = ("dma", o.stream) if o.is_dma else ("eng", o.eng)
            if o.signaled:
                cnt[key] = cnt.get(key, 0) + (16 if o.is_dma else 1)
                o.count = cnt[key]
                if key not in sems:
                    sems[key] = nc.alloc_semaphore(name="s_%s_%s" % key)
            else:
                o.count = None
        seen = {e: {} for e in self.ENGS}
        n_wait = 0
        for i, o in enumerate(ops):
            eng = self.e[o.eng]
            need = {}
            for d in o.deps:
                p = ops[d]
                key = ("dma", p.stream) if p.is_dma else ("eng", p.eng)
                if not p.is_dma and p.eng == o.eng and not ((SAME_ENGINE_SYNC and o.eng != "pe") or o.is_dma):
                    continue
                assert p.signaled
                if p.count > need.get(key, 0):
                    need[key] = p.count
            for key, c in need.items():
                if c > seen[o.eng].get(key, 0):
                    eng.wait_ge(sems[key], c)
                    seen[o.eng][key] = c
                    n_wait += 1
            ins = o.fn()
            if o.signaled:
                key = ("dma", o.stream) if o.is_dma else ("eng", o.eng)
                ins.then_inc(sems[key], 16 if o.is_dma else 1)
        for s in final_wait_streams:
            key = ("dma", s)
            if key in cnt:
                nc.sync.wait_ge(sems[key], cnt[key])
        self.stats = dict(n_ops=len(ops), n_wait=n_wait, n_sems=len(sems),
                          max_count=max(cnt.values()) if cnt else 0)
        return self.stats


class Ctx:
    pass


def mk_consts(ctx):
    nc, P = ctx.nc, ctx.P
    ctx.ones_bf = nc.alloc_sbuf_tensor("ones_bf", [128, 128], BF16)
    ctx.b_ones = Buf("ones")
    P.op("pool", lambda: nc.gpsimd.memset(ctx.ones_bf[:], 1.0), writes=[ctx.b_ones])
    ctx.eps_t = nc.alloc_sbuf_tensor("eps_t", [128, 1], F32)
    ctx.b_eps = Buf("eps")
    P.op("pool", lambda: nc.gpsimd.memset(ctx.eps_t[:], EPS), writes=[ctx.b_eps])


def convert_weight(ctx, src_ap, dst_ap, rows_per_dma):
    nc, P = ctx.nc, ctx.P
    R = src_ap.shape[0]
    b = Buf("cv")
    for r0 in range(0, R, rows_per_dma):
        r1 = min(R, r0 + rows_per_dma)
        P.op("pool", (lambda a=dst_ap[r0:r1], s=src_ap[r0:r1]: nc.gpsimd.dma_start(out=a, in_=s)),
             writes=[b], stream="cvt", accum=True)
    return b


def emit_ffn(ctx, x_in, x_out, T, g_ap, w_in_bf, w_out_bf, b_x_in, b_wcv, tag, pre_act=None, CC=JC, scale=0.5):
    nc, P = ctx.nc, ctx.P
    TT = 512
    assert T % TT == 0
    FW = 256
    NG = DFF // FW
    DW = 256
    NM = D // DW
    S = ctx.ffn_sb
    b_out = Buf("ffn_out_" + tag)
    for t in range(T // TT):
        t0 = t * TT
        if pre_act is not None:
            P.op("sp", lambda t0=t0: nc.sync.dma_start(
                out=S.x[:], in_=x_in.rearrange("(k p) t -> p k t", p=128)[:, :, t0:t0 + TT]),
                reads=[b_x_in], writes=[S.b_x], stream="ffn_x")
            P.op("sp", lambda t0=t0: nc.sync.dma_start(
                out=S.act[:, 0:CC, :], in_=pre_act.rearrange("(j p) t -> p j t", p=128)[:, :, t0:t0 + TT]),
                reads=[b_x_in], writes=[S.b_act], stream="ffn_pa")
        else:
            P.op("sp", lambda: nc.sync.dma_start(out=S.g[:], in_=g_ap.rearrange("(k p) -> p k", p=128),
                                                 allow_slow_non_contiguous=True),
                 reads=[b_wcv], writes=[S.b_g], stream="ffn_g")
            P.op("sp", lambda t0=t0: nc.sync.dma_start(
                out=S.x[:], in_=x_in.rearrange("(k p) t -> p k t", p=128)[:, :, t0:t0 + TT]),
                reads=[b_x_in], writes=[S.b_x], stream="ffn_x")
            for k in range(KC):
                sq = S.sq[k % 2]
                P.op("act", lambda k=k, sq=sq: nc.scalar.activation(out=sq[:], in_=S.x[:, k, :], func=AF.Square),
                     reads=[S.b_x], writes=[S.b_sq[k % 2]])
                P.op("pe", lambda k=k, sq=sq: nc.tensor.matmul(S.ps_ss[:], ctx.ones_bf[:], sq[:],
                                                             start=(k == 0), stop=(k == KC - 1)),
                     reads=[S.b_sq[k % 2], ctx.b_ones], writes=[S.b_ps_ss], accum=(k > 0))
            P.op("act", lambda: nc.scalar.activation(out=S.rstd[:], in_=S.ps_ss[:], func=AF.Sqrt,
                                                     bias=ctx.eps_t[:], scale=1.0 / D),
                 reads=[S.b_ps_ss, ctx.b_eps], writes=[S.b_rstd])
            P.op("dve", lambda: nc.vector.reciprocal(out=S.rstd[:], in_=S.rstd[:]),
                 reads=[S.b_rstd], writes=[S.b_rstd])
            for k in range(KC):
                eng = "dve"
                P.op(eng, lambda k=k, eng=eng: ctx.P.e[eng].scalar_tensor_tensor(
                    out=S.xn[:, k, :], in0=S.x[:, k, :], scalar=S.g[:, k:k + 1], in1=S.rstd[:],
                    op0=ALU.mult, op1=ALU.mult),
                    reads=[S.b_x, S.b_g, S.b_rstd], writes=[S.b_xn], accum=(k > 0))
            for gi in range(NG):
                wb = gi % 2
                f0 = gi * FW
                P.op("sp", lambda wb=wb, f0=f0: nc.sync.dma_start(
                    out=S.win[wb][:, 0, :, :], in_=w_in_bf.rearrange("(k p) f -> p k f", p=128)[:, :, f0:f0 + FW]),
                    reads=[b_wcv], writes=[S.b_win[wb]], stream="ffn_win%d" % wb)
                P.op("sp", lambda wb=wb, f0=f0: nc.sync.dma_start(
                    out=S.win[wb][:, 1, :, :],
                    in_=w_in_bf.rearrange("(k p) f -> p k f", p=128)[:, :, DFF + f0:DFF + f0 + FW]),
                    reads=[b_wcv], writes=[S.b_win[wb]], stream="ffn_win%d" % wb, accum=True)
                for jj in range(FW // 128):
                    j = gi * (FW // 128) + jj
                    pb = j % 2
                    for gu in range(2):
                        ps = S.ps_h[pb][gu]
                        for k in range(KC):
                            P.op("pe", lambda wb=wb, gu=gu, k=k, jj=jj, ps=ps: nc.tensor.matmul(
                                ps[:], S.win[wb][:, gu, k, jj * 128:(jj + 1) * 128], S.xn[:, k, :],
                                start=(k == 0), stop=(k == KC - 1)),
                                reads=[S.b_win[wb], S.b_xn], writes=[S.b_ps_h[pb][gu]], accum=(k > 0))
                    P.op("act", lambda pb=pb: nc.scalar.activation(out=S.silu[pb][:], in_=S.ps_h[pb][0][:], func=AF.Silu),
                         reads=[S.b_ps_h[pb][0]], writes=[S.b_silu[pb]])
                    P.op("dve", lambda pb=pb, j=j: nc.vector.tensor_tensor(
                        out=S.act[:, j, :], in0=S.silu[pb][:], in1=S.ps_h[pb][1][:], op=ALU.mult),
                        reads=[S.b_silu[pb], S.b_ps_h[pb][1]], writes=[S.b_act], accum=(j > 0))
        for mi in range(NM):
            wb = mi % 2
            d0 = mi * DW
            half = CC // 2
            P.op("sp", lambda wb=wb, d0=d0: nc.sync.dma_start(
                out=S.wout[wb][:, 0:half, :],
                in_=w_out_bf.rearrange("(j p) d -> p j d", p=128)[:, 0:half, d0:d0 + DW]),
                reads=[b_wcv], writes=[S.b_wout[wb]], stream="ffn_wout%d" % wb)
            P.op("sp", lambda wb=wb, d0=d0: nc.sync.dma_start(
                out=S.wout[wb][:, half:CC, :],
                in_=w_out_bf.rearrange("(j p) d -> p j d", p=128)[:, half:CC, d0:d0 + DW]),
                reads=[b_wcv], writes=[S.b_wout[wb]], stream="ffn_wout%d" % wb, accum=True)
            for mm in range(DW // 128):
                m = mi * (DW // 128) + mm
                pb = m % 2
                ps = S.ps_o[pb]
                for j in range(CC):
                    P.op("pe", lambda wb=wb, j=j, mm=mm, ps=ps: nc.tensor.matmul(
                        ps[:], S.wout[wb][:, j, mm * 128:(mm + 1) * 128], S.act[:, j, :],
                        start=(j == 0), stop=(j == CC - 1)),
                        reads=[S.b_wout[wb], S.b_act], writes=[S.b_ps_o[pb]], accum=(j > 0))
                P.op("dve", lambda m=m, ps=ps: nc.vector.scalar_tensor_tensor(
                    out=S.x[:, m, :], in0=ps[:], scalar=scale, in1=S.x[:, m, :], op0=ALU.mult, op1=ALU.add),
                    reads=[S.b_ps_o[pb], S.b_x], writes=[S.b_x], accum=True)
        P.op("sp", lambda t0=t0: nc.sync.dma_start(
            out=x_out.rearrange("(k p) t -> p k t", p=128)[:, :, t0:t0 + TT], in_=S.x[:]),
            reads=[S.b_x], writes=[b_out], stream="ffn_st", accum=True)
    return b_out


class FfnSb:
    def __init__(self, nc):
        TT = 512
        self.x = nc.alloc_sbuf_tensor("f_x", [128, KC, TT], F32)
        self.xn = nc.alloc_sbuf_tensor("f_xn", [128, KC, TT], BF16)
        self.act = nc.alloc_sbuf_tensor("f_act", [128, JC, TT], BF16)
        self.win = [nc.alloc_sbuf_tensor("f_win%d" % i, [128, 2, KC, 256], BF16) for i in range(2)]
        self.wout = [nc.alloc_sbuf_tensor("f_wout%d" % i, [128, JC, 256], BF16) for i in range(2)]
        self.sq = [nc.alloc_sbuf_tensor("f_sq%d" % i, [128, TT], BF16) for i in range(2)]
        self.silu = [nc.alloc_sbuf_tensor("f_silu%d" % i, [128, TT], F32) for i in range(2)]
        self.rstd = nc.alloc_sbuf_tensor("f_rstd", [128, TT], F32)
        self.g = nc.alloc_sbuf_tensor("f_g", [128, KC], F32)
        self.ps_ss = nc.alloc_psum_tensor("p_ss", [128, TT], F32)
        self.ps_h = [[nc.alloc_psum_tensor("p_h%d%d" % (i, j), [128, TT], F32) for j in range(2)] for i in range(2)]
        self.ps_o = [nc.alloc_psum_tensor("p_o%d" % i, [128, TT], F32) for i in range(2)]
        B = Buf
        self.b_x, self.b_xn, self.b_act = B("x"), B("xn"), B("act")
        self.b_win = [B("win0"), B("win1")]
        self.b_wout = [B("wout0"), B("wout1")]
        self.b_sq = [B("sq0"), B("sq1")]
        self.b_silu = [B("silu0"), B("silu1")]
        self.b_rstd, self.b_g, self.b_ps_ss = B("rstd"), B("g"), B("ps_ss")
        self.b_ps_h = [[B("ph00"), B("ph01")], [B("ph10"), B("ph11")]]
        self.b_ps_o = [B("po0"), B("po1")]


def host_consts():
    i = np.arange(128)
    u, t = i[:, None], i[None, :]
    c = np.zeros((128, 7, 128), np.float32)
    c[:, 0] = (u == t)
    c[:, 1] = (u <= t)
    c[:, 2] = (u >= t)
    c[:, 3] = (u > t)
    c[:, 4] = (u < t)
    c[:, 5] = -30000.0 * (t < u)
    c[:, 6] = -30000.0 * (t > u)
    return c


def load_consts(ctx, c_ap):
    nc, P = ctx.nc, ctx.P
    ctx.cf = nc.alloc_sbuf_tensor("cf", [128, 7, 128], F32)
    ctx.ident_bf = nc.alloc_sbuf_tensor("ident_bf", [128, 128], BF16)
    ctx.b_cf = Buf("cf")
    P.op("sp", lambda: nc.sync.dma_start(out=ctx.cf[:], in_=c_ap), writes=[ctx.b_cf], stream="cst")
    P.op("dve", lambda: nc.vector.tensor_copy(out=ctx.ident_bf[:], in_=ctx.cf[:, 0, :]),
         reads=[ctx.b_cf], writes=[ctx.b_cf], accum=True)


def emit_norm_tile(ctx, S, x_src_ap, g_ap, TT, b_src, eps_scale=None):
    nc, P = ctx.nc, ctx.P
    P.op("sp", lambda: nc.sync.dma_start(out=S.g[:], in_=g_ap.rearrange("(k p) -> p k", p=128),
                                         allow_slow_non_contiguous=True),
         writes=[S.b_g], stream="n_g")
    P.op("sp", lambda: nc.sync.dma_start(out=S.x[:, :, 0:TT], in_=x_src_ap), reads=[b_src], writes=[S.b_x], stream="n_x")
    for k in range(KC):
        sq = S.sq[k % 2]
        P.op("act", lambda k=k, sq=sq: nc.scalar.activation(out=sq[:, 0:TT], in_=S.x[:, k, 0:TT], func=AF.Square),
             reads=[S.b_x], writes=[S.b_sq[k % 2]])
        P.op("pe", lambda k=k, sq=sq: nc.tensor.matmul(S.ps_ss[:, 0:TT], ctx.ones_bf[:], sq[:, 0:TT],
                                                     start=(k == 0), stop=(k == KC - 1)),
             reads=[S.b_sq[k % 2], ctx.b_ones], writes=[S.b_ps_ss], accum=(k > 0))
    P.op("act", lambda: nc.scalar.activation(out=S.rstd[:, 0:TT], in_=S.ps_ss[:, 0:TT], func=AF.Sqrt,
                                             bias=ctx.eps_t[:], scale=1.0 / D),
         reads=[S.b_ps_ss, ctx.b_eps], writes=[S.b_rstd])
    P.op("dve", lambda: nc.vector.reciprocal(out=S.rstd[:, 0:TT], in_=S.rstd[:, 0:TT]),
         reads=[S.b_rstd], writes=[S.b_rstd])
    for k in range(KC):
        P.op("dve", lambda k=k: nc.vector.scalar_tensor_tensor(
            out=S.xn[:, k, 0:TT], in0=S.x[:, k, 0:TT], scalar=S.g[:, k:k + 1], in1=S.rstd[:, 0:TT],
            op0=ALU.mult, op1=ALU.mult),
            reads=[S.b_x, S.b_g, S.b_rstd], writes=[S.b_xn], accum=(k > 0))


class NormSb:
    def __init__(self, nc, TT, ps_ss):
        self.x = nc.alloc_sbuf_tensor("n_x", [128, KC, TT], F32)
        self.xn = nc.alloc_sbuf_tensor("n_xn", [128, KC, TT], BF16)
        self.sq = [nc.alloc_sbuf_tensor("n_sq%d" % i, [128, TT], BF16) for i in range(2)]
        self.rstd = nc.alloc_sbuf_tensor("n_rstd", [128, TT], F32)
        self.g = nc.alloc_sbuf_tensor("n_g", [128, KC], F32)
        self.ps_ss = ps_ss
        B = Buf
        self.b_x, self.b_xn, self.b_rstd, self.b_g, self.b_ps_ss = B("x"), B("xn"), B("rstd"), B("g"), B("pss")
        self.b_sq = [B("sq0"), B("sq1")]


def emit_proj_fm(ctx, S, wbuf, b_wbuf, w_bf, c0, ncols, ps, b_ps, TT, sink):
    nc, P = ctx.nc, ctx.P
    P.op("sp", lambda: nc.sync.dma_start(
        out=wbuf[:, :, 0:ncols], in_=w_bf.rearrange("(k p) f -> p k f", p=128)[:, :, c0:c0 + ncols]),
        reads=[ctx.b_wcv], writes=[b_wbuf], stream="wst_" + b_wbuf.name)
    for jj in range(ncols // 128):
        pi = jj % len(ps)
        for k in range(KC):
            P.op("pe", lambda k=k, jj=jj, pi=pi: nc.tensor.matmul(
                ps[pi][:, 0:TT], wbuf[:, k, jj * 128:(jj + 1) * 128], S.xn[:, k, 0:TT],
                start=(k == 0), stop=(k == KC - 1)),
                reads=[b_wbuf, S.b_xn], writes=[b_ps[pi]], accum=(k > 0))
        sink(jj, ps[pi], b_ps[pi])


def emit_proj_tm(ctx, S, wbuf, b_wbuf, w_bf, c0, ncols, ps, b_ps, TT, sink):
    nc, P = ctx.nc, ctx.P
    P.op("sp", lambda: nc.sync.dma_start(
        out=wbuf[:, :, 0:ncols], in_=w_bf.rearrange("(k p) f -> p k f", p=128)[:, :, c0:c0 + ncols]),
        reads=[ctx.b_wcv], writes=[b_wbuf], stream="wst_" + b_wbuf.name)
    for i in range(TT // 128):
        pi = i % len(ps)
        for k in range(KC):
            P.op("pe", lambda k=k, i=i, pi=pi: nc.tensor.matmul(
                ps[pi][:, 0:ncols], S.xn[:, k, i * 128:(i + 1) * 128], wbuf[:, k, 0:ncols],
                start=(k == 0), stop=(k == KC - 1)),
                reads=[b_wbuf, S.b_xn], writes=[b_ps[pi]], accum=(k > 0))
        sink(i, ps[pi], b_ps[pi])


class MemSb:
    def __init__(self, nc, NH):
        self.NH = NH
        self.KT = nc.alloc_sbuf_tensor("m_KT", [128, 2 * NH, 256], F32)
        self.KnT = nc.alloc_sbuf_tensor("m_KnT", [128, 2 * NH, 256], BF16)
        self.V = nc.alloc_sbuf_tensor("m_V", [128, 2, 256 * NH], BF16)
        self.kg = nc.alloc_sbuf_tensor("m_kg", [128, 2], F32)
        self.qg = nc.alloc_sbuf_tensor("m_qg", [128, 2], F32)
        self.qT = nc.alloc_sbuf_tensor("m_qT", [128, 2 * NH, 512], F32)
        self.qn = nc.alloc_sbuf_tensor("m_qn", [128, 2 * NH, 512], BF16)
        self.sq = [nc.alloc_sbuf_tensor("m_sq%d" % i, [128, 512], BF16) for i in range(2)]
        self.rs = nc.alloc_sbuf_tensor("m_rs", [128, 512], F32)
        self.pT = [nc.alloc_sbuf_tensor("m_pT%d" % i, [128, 512], BF16) for i in range(2)]
        self.rden = nc.alloc_sbuf_tensor("m_rden", [128, 512], F32)
        self.o = [nc.alloc_sbuf_tensor("m_o%d" % i, [128, 512], BF16) for i in range(2)]
        B = Buf
        self.b_KT, self.b_KnT, self.b_V, self.b_g = B("KT"), B("KnT"), B("V"), B("mg")
        self.b_qT, self.b_qn, self.b_rs, self.b_rden = B("qT"), B("qn"), B("rs"), B("rden")
        self.b_sq = [B("msq0"), B("msq1")]
        self.b_pT = [B("pT0"), B("pT1")]
        self.b_o = [B("mo0"), B("mo1")]


def emit_mem_prep(ctx, M, N, memT_ap, gmem_ap, wk_bf, wv_bf, kg_ap, qg_ap, wbuf, b_wbuf, ps, b_ps):
    nc, P = ctx.nc, ctx.P
    NH = M.NH
    b_in = Buf("memin")
    P.op("sp", lambda: nc.sync.dma_start(out=M.kg[:], in_=kg_ap.rearrange("(c p) -> p c", p=128),
                                         allow_slow_non_contiguous=True), writes=[M.b_g], stream="mg")
    P.op("sp", lambda: nc.sync.dma_start(out=M.qg[:], in_=qg_ap.rearrange("(c p) -> p c", p=128),
                                         allow_slow_non_contiguous=True), writes=[M.b_g], stream="mg", accum=True)
    emit_norm_tile(ctx, N, memT_ap.rearrange("(k p) t -> p k t", p=128), gmem_ap, 256, b_in)

    def ksink(jj, psap, bps):
        P.op("act", lambda: nc.scalar.copy(out=M.KT[:, jj, :], in_=psap[:, 0:256]),
             reads=[bps], writes=[M.b_KT], accum=(jj > 0))
    emit_proj_fm(ctx, N, wbuf, b_wbuf, wk_bf, 0, 256 * NH, ps, b_ps, 256, ksink)
    for hh in range(NH):
        for c in range(2):
            P.op("act", lambda c=c, hh=hh: nc.scalar.activation(out=M.sq[c][:, 0:256], in_=M.KT[:, 2 * hh + c, :],
                                                                func=AF.Square),
                 reads=[M.b_KT], writes=[M.b_sq[c]])
            P.op("pe", lambda c=c: nc.tensor.matmul(ps[0][:, 0:256], ctx.ones_bf[:], M.sq[c][:, 0:256],
                                                  start=(c == 0), stop=(c == 1)),
                 reads=[M.b_sq[c], ctx.b_ones], writes=[b_ps[0]], accum=(c > 0))
        P.op("act", lambda: nc.scalar.activation(out=M.rs[:, 0:256], in_=ps[0][:, 0:256], func=AF.Sqrt,
                                                 bias=ctx.eps_t[:], scale=1.0 / 256),
             reads=[b_ps[0], ctx.b_eps], writes=[M.b_rs])
        P.op("dve", lambda: nc.vector.reciprocal(out=M.rs[:, 0:256], in_=M.rs[:, 0:256]),
             reads=[M.b_rs], writes=[M.b_rs])
        for c in range(2):
            P.op("dve", lambda c=c, hh=hh: nc.vector.scalar_tensor_tensor(
                out=M.KnT[:, 2 * hh + c, :], in0=M.KT[:, 2 * hh + c, :], scalar=M.kg[:, c:c + 1],
                in1=M.rs[:, 0:256], op0=ALU.mult, op1=ALU.mult),
                reads=[M.b_KT, M.b_g, M.b_rs], writes=[M.b_KnT], accum=True)

    def vsink(i, psap, bps):
        P.op("act", lambda: nc.scalar.copy(out=M.V[:, i, :], in_=psap[:, 0:256 * NH]),
             reads=[bps], writes=[M.b_V], accum=True)
    emit_proj_tm(ctx, N, wbuf, b_wbuf, wv_bf, 0, 256 * NH, ps, b_ps, 256, vsink)


def emit_mem_attn_tile(ctx, M, ps, b_ps, out_ap_fn, b_out, stream):
    nc, P = ctx.nc, ctx.P
    NH = M.NH
    for hh in range(NH):
        for c in range(2):
            P.op("act", lambda c=c, hh=hh: nc.scalar.activation(out=M.sq[c][:], in_=M.qT[:, 2 * hh + c, :],
                                                                func=AF.Square),
                 reads=[M.b_qT], writes=[M.b_sq[c]])
            P.op("pe", lambda c=c: nc.tensor.matmul(ps[0][:], ctx.ones_bf[:], M.sq[c][:],
                                                  start=(c == 0), stop=(c == 1)),
                 reads=[M.b_sq[c], ctx.b_ones], writes=[b_ps[0]], accum=(c > 0))
        P.op("act", lambda: nc.scalar.activation(out=M.rs[:], in_=ps[0][:], func=AF.Sqrt,
                                                 bias=ctx.eps_t[:], scale=1.0 / 256),
             reads=[b_ps[0], ctx.b_eps], writes=[M.b_rs])
        P.op("dve", lambda: nc.vector.reciprocal(out=M.rs[:], in_=M.rs[:]), reads=[M.b_rs], writes=[M.b_rs])
        for c in range(2):
            P.op("dve", lambda c=c, hh=hh: nc.vector.scalar_tensor_tensor(
                out=M.qn[:, 2 * hh + c, :], in0=M.qT[:, 2 * hh + c, :], scalar=M.qg[:, c:c + 1],
                in1=M.rs[:], op0=ALU.mult, op1=ALU.mult),
                reads=[M.b_qT, M.b_g, M.b_rs], writes=[M.b_qn], accum=True)
        for mi in range(2):
            for c in range(2):
                P.op("pe", lambda c=c, mi=mi, hh=hh: nc.tensor.matmul(
                    ps[1][:], M.KnT[:, 2 * hh + c, mi * 128:(mi + 1) * 128], M.qn[:, 2 * hh + c, :],
                    start=(c == 0), stop=(c == 1)),
                    reads=[M.b_KnT, M.b_qn], writes=[b_ps[1]], accum=(c > 0))
            P.op("act", lambda mi=mi: nc.scalar.activation(out=M.pT[mi][:], in_=ps[1][:], func=AF.Exp, scale=1.0 / 16),
                 reads=[b_ps[1]], writes=[M.b_pT[mi]])
        for mi in range(2):
            P.op("pe", lambda mi=mi: nc.tensor.matmul(ps[0][:], ctx.ones_bf[:], M.pT[mi][:],
                                                    start=(mi == 0), stop=(mi == 1)),
                 reads=[M.b_pT[mi], ctx.b_ones], writes=[b_ps[0]], accum=(mi > 0))
        P.op("dve", lambda: nc.vector.reciprocal(out=M.rden[:], in_=ps[0][:]), reads=[b_ps[0]], writes=[M.b_rden])
        for dc in range(2):
            for mi in range(2):
                P.op("pe", lambda mi=mi, dc=dc, hh=hh: nc.tensor.matmul(
                    ps[1][:], M.V[:, mi, hh * 256 + dc * 128: hh * 256 + (dc + 1) * 128], M.pT[mi][:],
                    start=(mi == 0), stop=(mi == 1)),
                    reads=[M.b_V, M.b_pT[mi]], writes=[b_ps[1]], accum=(mi > 0))
            P.op("dve", lambda dc=dc: nc.vector.tensor_tensor(out=M.o[dc][:], in0=ps[1][:], in1=M.rden[:], op=ALU.mult),
                 reads=[b_ps[1], M.b_rden], writes=[M.b_o[dc]])
            P.op("sp", lambda dc=dc, hh=hh: nc.sync.dma_start(out=out_ap_fn(hh, dc), in_=M.o[dc][:]),
                 reads=[M.b_o[dc]], writes=[b_out], stream=stream + str(dc), accum=True)


SSD_NC = 4656


def bc3(ap2, n):
    return ap2.unsqueeze(2).to_broadcast([ap2.shape[0], ap2.shape[1], n])


def build_ssd(S, debug=False):
    nc = bass.Bass("TRN2", target_bir_lowering=False)
    dbg_streams = []

    def dbg(name, ap, buf, shape, dtype=F32):
        if not debug:
            return
        o = nc.dram_tensor("dbg_" + name, shape, dtype, kind="ExternalOutput").ap()
        P.op("sp", lambda: nc.sync.dma_start(out=o, in_=ap), reads=[buf], writes=[Buf("d")], stream="dbg_" + name)
        dbg_streams.append("dbg_" + name)
    dt_ = nc.dram_tensor
    xT = dt_("xT", [D, S], F32, kind="ExternalInput").ap()
    gmix = dt_("gmix", [D], F32, kind="ExternalInput").ap()
    wc = dt_("wc", [D, SSD_NC], F32, kind="ExternalInput").ap()
    conv_w = dt_("conv_w", [2560, 5], F32, kind="ExternalInput").ap()
    conv_b = dt_("conv_b", [2560], F32, kind="ExternalInput").ap()
    dt_bias = dt_("dt_bias", [1, 48], F32, kind="ExternalInput").ap()
    a_log = dt_("a_log", [1, 48], F32, kind="ExternalInput").ap()
    dskip = dt_("dskip", [1, 24], F32, kind="ExternalInput").ap()
    norm_g = dt_("norm_g", [1, 1536], F32, kind="ExternalInput").ap()
    memT = dt_("memT", [D, 256], F32, kind="ExternalInput").ap()
    gmem = dt_("gmem", [D], F32, kind="ExternalInput").ap()
    wk = dt_("wk", [D, 512], F32, kind="ExternalInput").ap()
    wv = dt_("wv", [D, 512], F32, kind="ExternalInput").ap()
    kg = dt_("kg", [256], F32, kind="ExternalInput").ap()
    qg = dt_("qg", [256], F32, kind="ExternalInput").ap()
    cst = dt_("cst", [128, 7, 128], F32, kind="ExternalInput").ap()
    yT = dt_("yT", [D, S], BF16, kind="ExternalOutput").ap()
    wc_bf = dt_("wc_bf", [D, SSD_NC], BF16, kind="Internal").ap()
    wk_bf = dt_("wk_bf", [D, 512], BF16, kind="Internal").ap()
    wv_bf = dt_("wv_bf", [D, 512], BF16, kind="Internal").ap()
    xbc_pre = dt_("xbc_pre", [2560, S], F32, kind="Internal").ap()
    zs = dt_("zs", [S, 1536], BF16, kind="Internal").ap()
    dtr = dt_("dtr", [S, 48], F32, kind="Internal").ap()
    y1 = dt_("y1", [S, 1536], F32, kind="Internal").ap()

    ctx = Ctx(); ctx.nc = nc; P = ctx.P = Prog(nc)
    mk_consts(ctx)
    load_consts(ctx, cst)
    ones_f = nc.alloc_sbuf_tensor("ones_f", [128, 128], F32)
    P.op("pool", lambda: nc.gpsimd.memset(ones_f[:], 1.0), writes=[ctx.b_ones], accum=True)
    convert_weight(ctx, wc, wc_bf, 256)
    convert_weight(ctx, wk, wk_bf, 512)
    convert_weight(ctx, wv, wv_bf, 512)
    ctx.b_wcv = Buf("wcv")
    P.barrier()

    PS = [nc.alloc_psum_tensor("ps%d" % i, [128, 512], F32) for i in range(7)]
    PSB = nc.alloc_psum_tensor("psb", [128, 1024], BF16)
    bPS = [Buf("ps%d" % i) for i in range(7)]
    bPSB = Buf("psb")
    N = NormSb(nc, 512, PS[0])
    N.b_ps_ss = bPS[0]
    M = MemSb(nc, 2)
    wbuf = [nc.alloc_sbuf_tensor("wbuf%d" % i, [128, KC, 512], BF16) for i in range(2)]
    b_wbuf = [Buf("wb0"), Buf("wb1")]
    stg = nc.alloc_sbuf_tensor("stg", [128, 4, 512], F32)
    b_stg = Buf("stg")
    zst = nc.alloc_sbuf_tensor("zst", [128, 4, 512], BF16)
    b_zst = Buf("zst")
    dst = nc.alloc_sbuf_tensor("dst", [128, 4, 48], F32)
    b_dst = Buf("dst")
    b_xin = Buf("xin")
    b_pre_d, b_zs_d, b_dtr_d, b_y1_d, b_out = Buf("pre_d"), Buf("zs_d"), Buf("dtr_d"), Buf("y1_d"), Buf("out")

    emit_mem_prep(ctx, M, N, memT, gmem, wk_bf, wv_bf, kg, qg, wbuf[0], b_wbuf[0], [PS[1], PS[2]], [bPS[1], bPS[2]])

    TT = 512
    yT_v = yT.rearrange("(j p) t -> p j t", p=128)
    pre_v = xbc_pre.rearrange("(j p) t -> p j t", p=128)
    zs_v = zs.rearrange("(i p) f -> p i f", p=128)
    dtr_v = dtr.rearrange("(i p) f -> p i f", p=128)
    y1_v = y1.rearrange("(i p) f -> p i f", p=128)
    wi = [0]

    def nextw():
        wi[0] += 1
        return wbuf[wi[0] % 2], b_wbuf[wi[0] % 2]

    for t in range(S // TT):
        t0 = t * TT
        emit_norm_tile(ctx, N, xT.rearrange("(k p) t -> p k t", p=128)[:, :, t0:t0 + TT], gmix, TT, b_xin)
        for gi in range(5):
            wb, bwb = nextw()

            def sink(jj, psap, bps):
                P.op("act", lambda: nc.scalar.copy(out=stg[:, jj, :], in_=psap[:]),
                     reads=[bps], writes=[b_stg], accum=(jj > 0))
            emit_proj_fm(ctx, N, wb, bwb, wc_bf, 1536 + gi * 512, 512, [PS[1], PS[2]], [bPS[1], bPS[2]], TT, sink)
            P.op("sp", lambda gi=gi, t0=t0: nc.sync.dma_start(out=pre_v[:, gi * 4:(gi + 1) * 4, t0:t0 + TT], in_=stg[:]),
                 reads=[b_stg], writes=[b_pre_d], stream="st_pre", accum=True)
        wb, bwb = nextw()

        def qsink(jj, psap, bps):
            P.op("act", lambda: nc.scalar.copy(out=M.qT[:, jj, :], in_=psap[:]),
                 reads=[bps], writes=[M.b_qT], accum=(jj > 0))
        emit_proj_fm(ctx, N, wb, bwb, wc_bf, 4144, 512, [PS[1], PS[2]], [bPS[1], bPS[2]], TT, qsink)
        emit_mem_attn_tile(ctx, M, [PS[3], PS[4]], [bPS[3], bPS[4]],
                           lambda hh, dc, t0=t0: yT_v[:, 12 + hh * 2 + dc, t0:t0 + TT], b_out, "st_mo")
        for zi in range(3):
            wb, bwb = nextw()

            def zsink(i, psap, bps):
                P.op("act", lambda: nc.scalar.activation(out=zst[:, i, :], in_=psap[:], func=AF.Silu),
                     reads=[bps], writes=[b_zst], accum=(i > 0))
            emit_proj_tm(ctx, N, wb, bwb, wc_bf, zi * 512, 512, [PS[1], PS[2]], [bPS[1], bPS[2]], TT, zsink)
            P.op("sp", lambda zi=zi, t0=t0: nc.sync.dma_start(
                out=zs_v[:, t0 // 128:t0 // 128 + 4, zi * 512:(zi + 1) * 512], in_=zst[:]),
                reads=[b_zst], writes=[b_zs_d], stream="st_zs", accum=True)
        wb, bwb = nextw()

        def dsink(i, psap, bps):
            P.op("act", lambda: nc.scalar.copy(out=dst[:, i, :], in_=psap[:, 0:48]),
                 reads=[bps], writes=[b_dst], accum=(i > 0))
        emit_proj_tm(ctx, N, wb, bwb, wc_bf, 4096, 48, [PS[1], PS[2]], [bPS[1], bPS[2]], TT, dsink)
        P.op("sp", lambda t0=t0: nc.sync.dma_start(out=dtr_v[:, t0 // 128:t0 // 128 + 4, :], in_=dst[:]),
             reads=[b_dst], writes=[b_dtr_d], stream="st_dt", accum=True)

    P.barrier()

    sb = nc.alloc_sbuf_tensor
    pre = sb("pre", [128, 20, 132], F32)
    acc = sb("acc", [128, 20, 128], F32)
    fm = sb("fm", [128, 20, 128], BF16)
    xs_tok = sb("xs_tok", [128, 1536], BF16)
    B_tok = sb("B_tok", [128, 512], BF16)
    cw = sb("cw", [128, 20, 5], F32)
    cb = sb("cb", [128, 20], F32)
    dtb = sb("dtb", [128, 48], F32)
    A_bc = sb("A_bc", [128, 48], F32)
    D_bc = sb("D_bc", [128, 24], F32)
    ng_bc = sb("ng_bc", [128, 1536], F32)
    dtraw = sb("dtraw", [128, 48], F32)
    dtx = sb("dtx", [128, 24], F32)
    dtv = sb("dtv", [128, 24], F32)
    av = sb("av", [128, 24], F32)
    cs_sb = sb("cs_sb", [128, 24], F32)
    dout = sb("dout", [128, 24], F32)
    din = sb("din", [128, 24], F32)
    cdec = sb("cdec", [128, 24], F32)
    X = sb("X", [128, 24, 64], BF16)
    Xd = sb("Xd", [128, 24, 64], BF16)
    cbs = sb("cbs", [128, 4, 128], F32)
    la = [sb("la%d" % i, [128, 3, 128], F32) for i in range(2)]
    E = [sb("E%d" % i, [128, 3, 128], F32) for i in range(2)]
    MT = [sb("MT%d" % i, [128, 3, 128], BF16) for i in range(2)]
    t1 = sb("t1", [128, 6, 64], F32)
    xv = lambda i: N.x[:, 3 * i:3 * i + 3, :].rearrange("p k t -> p (k t)")
    ypart = xv(0)
    y1t = xv(1)
    zt = sb("zt", [128, 1536], BF16)
    t2 = xv(2)
    ss4 = sb("ss4", [128, 4], F32)
    yn = sb("yn", [128, 1536], BF16)
    yTt = sb("yTt", [128, 12, 128], BF16)
    H = [xv(3), xv(4)]
    Hbf = [sb("Hbf%d" % i, [128, 1536], BF16) for i in range(2)]
    B_ = Buf
    b = {n: B_(n) for n in ["pre", "acc", "fm", "xs_tok", "B_tok", "par", "dtraw", "dtx", "dtv", "av", "cs_sb", "dout",
                            "din", "cdec", "X", "Xd", "cbs", "la0", "la1", "E0", "E1", "MT0", "MT1", "t1", "ypart",
                            "y1t", "zt", "t2", "ss4", "yn", "yTt", "H0", "H1", "Hbf0", "Hbf1"]}
    P.op("sp", lambda: nc.sync.dma_start(out=cw[:], in_=conv_w.rearrange("(j p) k -> p j k", p=128),
                                         allow_slow_non_contiguous=True), writes=[b["par"]], stream="par")
    P.op("sp", lambda: nc.sync.dma_start(out=cb[:], in_=conv_b.rearrange("(j p) -> p j", p=128),
                                         allow_slow_non_contiguous=True), writes=[b["par"]], stream="par", accum=True)
    for (dst_t, src, n) in [(dtb, dt_bias, 48), (A_bc, a_log, 48), (D_bc, dskip, 24), (ng_bc, norm_g, 1536)]:
        P.op("sp", lambda dst_t=dst_t, src=src, n=n: nc.sync.dma_start(out=dst_t[:], in_=src.to_broadcast([128, n])),
             writes=[b["par"]], stream="par", accum=True)
    P.op("act", lambda: nc.scalar.activation(out=A_bc[:], in_=A_bc[:], func=AF.Exp), reads=[b["par"]], writes=[b["par"]])
    P.op("dve", lambda: nc.vector.tensor_scalar(out=A_bc[:], in0=A_bc[:], scalar1=-1.0, scalar2=None, op0=ALU.mult),
         reads=[b["par"]], writes=[b["par"]])

    NCH = S // 128
    ps_small, b_small = PS[0], bPS[0]
    ps_cb, b_cb = PS[1], bPS[1]
    ps_dd, b_dd = [PS[2], PS[3]], [bPS[2], bPS[3]]
    ps_yd, b_yd = PS[4], bPS[4]
    ps_yo, b_yo = PS[5], bPS[5]
    ps_st, b_st = PS[6], bPS[6]

    for d in range(2):
        P.op("pool", lambda d=d: nc.gpsimd.memset(H[d][:], 0.0), writes=[b["H%d" % d]])
        P.op("pool", lambda d=d: nc.gpsimd.memset(Hbf[d][:], 0.0), writes=[b["Hbf%d" % d]])
        order = range(NCH) if d == 0 else range(NCH - 1, -1, -1)
        for c in order:
            lo, hi = max(0, c * 128 - 2), min(S, c * 128 + 130)
            off = lo - (c * 128 - 2)
            first = True
            if c == 0:
                P.op("pool", lambda: nc.gpsimd.memset(pre[:, :, 0:2], 0.0), writes=[b["pre"]])
                first = False
            if c == NCH - 1:
                P.op("pool", lambda: nc.gpsimd.memset(pre[:, :, 130:132], 0.0), writes=[b["pre"]], accum=not first)
                first = False
            P.op("sp", lambda lo=lo, hi=hi, off=off: nc.sync.dma_start(out=pre[:, :, off:off + hi - lo], in_=pre_v[:, :, lo:hi]),
                 reads=[b_pre_d], writes=[b["pre"]], stream="ld_pre", accum=not first)
            P.op("sp", lambda c=c: nc.sync.dma_start(out=dtraw[:], in_=dtr[c * 128:(c + 1) * 128, :]),
                 reads=[b_dtr_d], writes=[b["dtraw"]], stream="ld_dt")
            for j in range(20):
                P.op("dve", lambda j=j: nc.vector.tensor_scalar(
                    out=acc[:, j, :], in0=pre[:, j, 0:128], scalar1=cw[:, j, 0:1], scalar2=cb[:, j:j + 1],
                    op0=ALU.mult, op1=ALU.add), reads=[b["pre"], b["par"]], writes=[b["acc"]], accum=(j > 0))
                for k in range(1, 5):
                    P.op("dve", lambda j=j, k=k: nc.vector.scalar_tensor_tensor(
                        out=acc[:, j, :], in0=pre[:, j, k:k + 128], scalar=cw[:, j, k:k + 1], in1=acc[:, j, :],
                        op0=ALU.mult, op1=ALU.add), reads=[b["pre"], b["par"]], writes=[b["acc"]], accum=True)
            P.op("act", lambda: nc.scalar.activation(out=fm[:], in_=acc[:], func=AF.Silu),
                 reads=[b["acc"]], writes=[b["fm"]])
            for jb in range(4):
                for q in range(4):
                    P.op("pe", lambda jb=jb, q=q: nc.tensor.transpose(
                        out=PSB[:, q * 128:(q + 1) * 128], in_=fm[:, jb * 4 + q, :], identity=ctx.ident_bf[:]),
                        reads=[b["fm"], ctx.b_cf], writes=[bPSB], accum=(q > 0))
                dstt = xs_tok[:, jb * 512:(jb + 1) * 512] if jb < 3 else B_tok[:]
                P.op("act", lambda dstt=dstt: nc.scalar.copy(out=dstt, in_=PSB[:, 0:512]),
                     reads=[bPSB], writes=[b["xs_tok"] if jb < 3 else b["B_tok"]], accum=(0 < jb < 3))
            P.op("dve", lambda d=d: nc.vector.tensor_tensor(out=dtx[:], in0=dtraw[:, d * 24:(d + 1) * 24],
                                                            in1=dtb[:, d * 24:(d + 1) * 24], op=ALU.add),
                 reads=[b["dtraw"], b["par"]], writes=[b["dtx"]])
            P.op("act", lambda: nc.scalar.activation(out=dtx[:], in_=dtx[:], func=AF.Exp), reads=[b["dtx"]], writes=[b["dtx"]])
            P.op("act", lambda: nc.scalar.activation(out=dtv[:], in_=dtx[:], func=AF.Ln, bias=1.0),
                 reads=[b["dtx"]], writes=[b["dtv"]])
            P.op("dve", lambda d=d: nc.vector.tensor_tensor(out=av[:], in0=dtv[:], in1=A_bc[:, d * 24:(d + 1) * 24], op=ALU.mult),
                 reads=[b["dtv"], b["par"]], writes=[b["av"]])
            P.op("pe", lambda d=d: nc.tensor.matmul(ps_small[:, 0:24], ctx.cf[:, 1 + d, :], av[:], start=True, stop=True),
                 reads=[b["av"], ctx.b_cf], writes=[b_small])
            P.op("pe", lambda: nc.tensor.matmul(ps_small[:, 32:56], ones_f[:], av[:], start=True, stop=True),
                 reads=[b["av"], ctx.b_ones], writes=[b_small], accum=True)
            P.op("act", lambda: nc.scalar.copy(out=cs_sb[:], in_=ps_small[:, 0:24]), reads=[b_small], writes=[b["cs_sb"]])
            P.op("act", lambda: nc.scalar.activation(out=dout[:], in_=ps_small[:, 0:24], func=AF.Exp),
                 reads=[b_small], writes=[b["dout"]])
            P.op("act", lambda: nc.scalar.activation(out=cdec[:], in_=ps_small[:, 32:56], func=AF.Exp),
                 reads=[b_small], writes=[b["cdec"]])
            P.op("dve", lambda: nc.vector.tensor_tensor(out=din[:], in0=ps_small[:, 32:56], in1=cs_sb[:], op=ALU.subtract),
                 reads=[b_small, b["cs_sb"]], writes=[b["din"]])
            P.op("act", lambda: nc.scalar.activation(out=din[:], in_=din[:], func=AF.Exp), reads=[b["din"]], writes=[b["din"]])
            P.op("dve", lambda: nc.vector.tensor_tensor(out=X[:], in0=xs_tok[:].rearrange("p (h e) -> p h e", e=64),
                                                        in1=bc3(dtv[:], 64), op=ALU.mult),
                 reads=[b["xs_tok"], b["dtv"]], writes=[b["X"]])
            P.op("dve", lambda: nc.vector.tensor_tensor(out=Xd[:], in0=X[:], in1=bc3(din[:], 64), op=ALU.mult),
                 reads=[b["X"], b["din"]], writes=[b["Xd"]])
            for g in range(4):
                P.op("pe", lambda g=g: nc.tensor.matmul(ps_cb[:, g * 128:(g + 1) * 128], fm[:, 12 + g, :], fm[:, 16 + g, :],
                                                        start=True, stop=True),
                     reads=[b["fm"]], writes=[b_cb], accum=(g > 0))
            P.op("act", lambda: nc.scalar.copy(out=cbs[:], in_=ps_cb[:].rearrange("p (g l) -> p g l", g=4)),
                 reads=[b_cb], writes=[b["cbs"]])
            for g in range(4):
                for hb in range(2):
                    for r3 in range(3):
                        h = g * 6 + hb * 3 + r3
                        P.op("dve", lambda hb=hb, r3=r3, h=h, d=d: nc.vector.tensor_scalar(
                            out=la[hb][:, r3, :], in0=ctx.cf[:, 3 + d, :], scalar1=av[:, h:h + 1], scalar2=None, op0=ALU.mult),
                            reads=[b["av"], ctx.b_cf], writes=[b["la%d" % hb]], accum=(r3 > 0))
                    for r3 in range(3):
                        P.op("pe", lambda hb=hb, r3=r3, d=d: nc.tensor.matmul(
                            ps_dd[hb][:, r3 * 128:(r3 + 1) * 128], la[hb][:, r3, :], ctx.cf[:, 1 + d, :], start=True, stop=False),
                            reads=[b["la%d" % hb], ctx.b_cf], writes=[b_dd[hb]], accum=(r3 > 0))
                        P.op("pe", lambda hb=hb, r3=r3, d=d: nc.tensor.matmul(
                            ps_dd[hb][:, r3 * 128:(r3 + 1) * 128], ctx.cf[:, 0, :], ctx.cf[:, 5 + d, :], start=False, stop=True),
                            reads=[ctx.b_cf], writes=[b_dd[hb]], accum=True)
                    P.op("act", lambda hb=hb: nc.scalar.activation(
                        out=E[hb][:], in_=ps_dd[hb][:, 0:384].rearrange("p (r l) -> p r l", r=3), func=AF.Exp),
                        reads=[b_dd[hb]], writes=[b["E%d" % hb]])
                    P.op("dve", lambda hb=hb, g=g: nc.vector.tensor_tensor(
                        out=MT[hb][:], in0=E[hb][:], in1=cbs[:, g, :].unsqueeze(1).to_broadcast([128, 3, 128]), op=ALU.mult),
                        reads=[b["E%d" % hb], b["cbs"]], writes=[b["MT%d" % hb]])
                    for r3 in range(3):
                        h = g * 6 + hb * 3 + r3
                        P.op("pe", lambda hb=hb, r3=r3, h=h: nc.tensor.matmul(
                            ps_yd[:, (hb * 3 + r3) * 64:(hb * 3 + r3 + 1) * 64], MT[hb][:, r3, :], X[:, h, :], start=True, stop=True),
                            reads=[b["MT%d" % hb], b["X"]], writes=[b_yd], accum=not (hb == 0 and r3 == 0))
                P.op("pe", lambda g=g, d=d: nc.tensor.matmul(ps_yo[:, 0:384], fm[:, 16 + g, :], Hbf[d][:, g * 384:(g + 1) * 384],
                                                             start=True, stop=True),
                     reads=[b["fm"], b["Hbf%d" % d]], writes=[b_yo])
                P.op("dve", lambda g=g: nc.vector.tensor_tensor(
                    out=t1[:], in0=ps_yo[:, 0:384].rearrange("p (h e) -> p h e", e=64), in1=bc3(dout[:, g * 6:(g + 1) * 6], 64),
                    op=ALU.mult), reads=[b_yo, b["dout"]], writes=[b["t1"]])
                P.op("dve", lambda g=g: nc.vector.tensor_tensor(
                    out=ypart[:, g * 384:(g + 1) * 384], in0=t1[:].rearrange("p h e -> p (h e)"), in1=ps_yd[:, 0:384], op=ALU.add),
                    reads=[b["t1"], b_yd], writes=[b["ypart"]], accum=(g > 0))
                P.op("pe", lambda g=g: nc.tensor.matmul(ps_st[:, 0:384], B_tok[:, g * 128:(g + 1) * 128],
                                                        Xd[:, g * 6:(g + 1) * 6, :].rearrange("p h e -> p (h e)"), start=True, stop=True),
                     reads=[b["B_tok"], b["Xd"]], writes=[b_st])
                Hg = H[d][:, g * 384:(g + 1) * 384]
                P.op("dve", lambda Hg=Hg, g=g: nc.vector.tensor_tensor(
                    out=Hg.rearrange("p (h e) -> p h e", e=64), in0=Hg.rearrange("p (h e) -> p h e", e=64),
                    in1=bc3(cdec[:, g * 6:(g + 1) * 6], 64), op=ALU.mult),
                    reads=[b["H%d" % d], b["cdec"]], writes=[b["H%d" % d]])
                P.op("dve", lambda Hg=Hg: nc.vector.tensor_tensor(out=Hg, in0=Hg, in1=ps_st[:, 0:384], op=ALU.add),
                     reads=[b["H%d" % d], b_st], writes=[b["H%d" % d]])
                P.op("act", lambda Hg=Hg, g=g, d=d: nc.scalar.copy(out=Hbf[d][:, g * 384:(g + 1) * 384], in_=Hg),
                     reads=[b["H%d" % d]], writes=[b["Hbf%d" % d]])
            if d == 0 and c == 0:
                dbg("dtraw", dtraw[:], b["dtraw"], [128, 48])
                dbg("dtv", dtv[:], b["dtv"], [128, 24])
                dbg("av", av[:], b["av"], [128, 24])
                dbg("cs", cs_sb[:], b["cs_sb"], [128, 24])
                dbg("dout", dout[:], b["dout"], [128, 24])
                dbg("din", din[:], b["din"], [128, 24])
                dbg("cdec", cdec[:], b["cdec"], [128, 24])
                dbg("cbs", cbs[:], b["cbs"], [128, 4, 128])
                dbg("E1", E[1][:], b["E1"], [128, 3, 128])
                dbg("xs", xs_tok[:], b["xs_tok"], [128, 1536], BF16)
                dbg("fm", fm[:], b["fm"], [128, 20, 128], BF16)
                dbg("acc", acc[:], b["acc"], [128, 20, 128])
                dbg("ypart", ypart[:], b["ypart"], [128, 1536])
                dbg("H0", H[0][:], b["H0"], [128, 1536])
            if d == 0:
                P.op("sp", lambda c=c: nc.sync.dma_start(out=y1[c * 128:(c + 1) * 128, :], in_=ypart[:]),
                     reads=[b["ypart"]], writes=[b_y1_d], stream="st_y1", accum=True)
            else:
                P.op("sp", lambda c=c: nc.sync.dma_start(out=y1t[:], in_=y1[c * 128:(c + 1) * 128, :]),
                     reads=[b_y1_d], writes=[b["y1t"]], stream="ld_y1")
                P.op("sp", lambda c=c: nc.sync.dma_start(out=zt[:], in_=zs[c * 128:(c + 1) * 128, :]),
                     reads=[b_zs_d], writes=[b["zt"]], stream="ld_zs")
                P.op("dve", lambda: nc.vector.tensor_tensor(out=ypart[:], in0=ypart[:], in1=y1t[:], op=ALU.add),
                     reads=[b["ypart"], b["y1t"]], writes=[b["ypart"]])
                P.op("dve", lambda: nc.vector.tensor_tensor(
                    out=t2[:].rearrange("p (h e) -> p h e", e=64), in0=xs_tok[:].rearrange("p (h e) -> p h e", e=64),
                    in1=bc3(D_bc[:], 64), op=ALU.mult), reads=[b["xs_tok"], b["par"]], writes=[b["t2"]])
                P.op("dve", lambda: nc.vector.tensor_tensor(out=ypart[:], in0=ypart[:], in1=t2[:], op=ALU.add),
                     reads=[b["ypart"], b["t2"]], writes=[b["ypart"]])
                P.op("dve", lambda: nc.vector.tensor_tensor(out=ypart[:], in0=ypart[:], in1=zt[:], op=ALU.mult),
                     reads=[b["ypart"], b["zt"]], writes=[b["ypart"]])
                P.op("act", lambda: nc.scalar.activation(out=t2[:], in_=ypart[:], func=AF.Square),
                     reads=[b["ypart"]], writes=[b["t2"]])
                P.op("dve", lambda: nc.vector.tensor_reduce(out=ss4[:], in_=t2[:].rearrange("p (g f) -> p g f", g=4),
                                                            axis=AX.X, op=ALU.add), reads=[b["t2"]], writes=[b["ss4"]])
                P.op("act", lambda: nc.scalar.activation(out=ss4[:], in_=ss4[:], func=AF.Sqrt, bias=ctx.eps_t[:], scale=1.0 / 384),
                     reads=[b["ss4"], ctx.b_eps], writes=[b["ss4"]])
                P.op("dve", lambda: nc.vector.reciprocal(out=ss4[:], in_=ss4[:]), reads=[b["ss4"]], writes=[b["ss4"]])
                P.op("dve", lambda: nc.vector.tensor_tensor(
                    out=t2[:].rearrange("p (g f) -> p g f", g=4), in0=ypart[:].rearrange("p (g f) -> p g f", g=4),
                    in1=bc3(ss4[:], 384), op=ALU.mult), reads=[b["ypart"], b["ss4"]], writes=[b["t2"]])
                P.op("dve", lambda: nc.vector.tensor_tensor(out=yn[:], in0=t2[:], in1=ng_bc[:], op=ALU.mult),
                     reads=[b["t2"], b["par"]], writes=[b["yn"]])
                for jb in range(3):
                    for q in range(4):
                        P.op("pe", lambda jb=jb, q=q: nc.tensor.transpose(
                            out=PSB[:, q * 128:(q + 1) * 128], in_=yn[:, (jb * 4 + q) * 128:(jb * 4 + q + 1) * 128],
                            identity=ctx.ident_bf[:]), reads=[b["yn"], ctx.b_cf], writes=[bPSB], accum=(q > 0))
                    P.op("act", lambda jb=jb: nc.scalar.copy(out=yTt[:, jb * 4:(jb + 1) * 4, :],
                                                             in_=PSB[:, 0:512].rearrange("p (q t) -> p q t", q=4)),
                         reads=[bPSB], writes=[b["yTt"]], accum=(jb > 0))
                P.op("sp", lambda c=c: nc.sync.dma_start(out=yT_v[:, 0:12, c * 128:(c + 1) * 128], in_=yTt[:]),
                     reads=[b["yTt"]], writes=[b_out], stream="st_y", accum=True)
    st = P.emit(final_wait_streams=["st_y", "st_mo0", "st_mo1"] + dbg_streams)
    return nc, st


def emit_transpose_in(ctx, S, x_tok, xT, T, PSb, bPSb, b_out):
    nc, P = ctx.nc, ctx.P
    for t in range(T // 512):
        for i in range(4):
            r0 = t * 512 + i * 128
            P.op("sp", lambda r0=r0: nc.sync.dma_start(out=ctx.tok[:], in_=x_tok[r0:r0 + 128, :]),
                 writes=[ctx.b_tok], stream="ti_ld")
            for kb in range(4):
                pi = kb % 2
                for q in range(4):
                    k = kb * 4 + q
                    P.op("pe", lambda k=k, q=q, pi=pi: nc.tensor.transpose(
                        out=PSb[pi][:, q * 128:(q + 1) * 128], in_=ctx.tok[:, k * 128:(k + 1) * 128], identity=ctx.cf[:, 0, :]),
                        reads=[ctx.b_tok, ctx.b_cf], writes=[bPSb[pi]], accum=(q > 0))
                P.op("act", lambda kb=kb, i=i, pi=pi: nc.scalar.copy(
                    out=S.x[:, kb * 4:(kb + 1) * 4, i * 128:(i + 1) * 128],
                    in_=PSb[pi][:].rearrange("p (q t) -> p q t", q=4)),
                    reads=[bPSb[pi]], writes=[S.b_x], accum=not (i == 0 and kb == 0))
        P.op("sp", lambda t=t: nc.sync.dma_start(
            out=xT.rearrange("(k p) t -> p k t", p=128)[:, :, t * 512:(t + 1) * 512], in_=S.x[:]),
            reads=[S.b_x], writes=[b_out], stream="ti_st", accum=True)


def emit_transpose_out(ctx, S, xT, out_tok, T, PSb, bPSb, b_in):
    nc, P = ctx.nc, ctx.P
    b_o = Buf("final")
    for t in range(T // 512):
        P.op("sp", lambda t=t: nc.sync.dma_start(
            out=S.x[:], in_=xT.rearrange("(k p) t -> p k t", p=128)[:, :, t * 512:(t + 1) * 512]),
            reads=[b_in], writes=[S.b_x], stream="to_ld")
        for i in range(4):
            for kb in range(4):
                pi = kb % 2
                for q in range(4):
                    k = kb * 4 + q
                    P.op("pe", lambda k=k, q=q, pi=pi, i=i: nc.tensor.transpose(
                        out=PSb[pi][:, q * 128:(q + 1) * 128], in_=S.x[:, k, i * 128:(i + 1) * 128], identity=ctx.cf[:, 0, :]),
                        reads=[S.b_x, ctx.b_cf], writes=[bPSb[pi]], accum=(q > 0))
                P.op("act", lambda kb=kb, pi=pi: nc.scalar.copy(out=ctx.tok[:, kb * 512:(kb + 1) * 512], in_=PSb[pi][:]),
                     reads=[bPSb[pi]], writes=[ctx.b_tok], accum=(kb > 0))
            r0 = t * 512 + i * 128
            P.op("sp", lambda r0=r0: nc.sync.dma_start(out=out_tok[r0:r0 + 128, :], in_=ctx.tok[:]),
                 reads=[ctx.b_tok], writes=[b_o], stream="to_st", accum=True)


def build_tok(T, n_ffn, tin, tout, cin):
    nc = bass.Bass("TRN2", target_bir_lowering=False)
    dt_ = nc.dram_tensor
    ctx = Ctx(); ctx.nc = nc; P = ctx.P = Prog(nc)
    cst = dt_("cst", [128, 7, 128], F32, kind="ExternalInput").ap()
    if tin:
        x_in = dt_("x", [T, D], F32, kind="ExternalInput").ap()
    else:
        x_in = dt_("xT", [D, T], F32, kind="ExternalInput").ap()
    if tout:
        out = dt_("out", [T, D], F32, kind="ExternalOutput").ap()
    else:
        out = dt_("outT", [D, T], F32, kind="ExternalOutput").ap()
    mk_consts(ctx)
    load_consts(ctx, cst)
    S = ctx.ffn_sb = FfnSb(nc)
    ctx.tok = nc.alloc_sbuf_tensor("tok", [128, D], F32)
    ctx.b_tok = Buf("tok")
    b_w = Buf("w")
    ws = []
    if cin:
        yT = dt_("yT", [cin, T], BF16, kind="ExternalInput").ap()
        w_o = dt_("w_o", [cin, D], F32, kind="ExternalInput").ap()
        w_o_bf = dt_("w_o_bf", [cin, D], BF16, kind="Internal").ap()
        convert_weight(ctx, w_o, w_o_bf, 512)
    for f in range(n_ffn):
        g = dt_("g%d" % f, [D], F32, kind="ExternalInput").ap()
        wi = dt_("wi%d" % f, [D, 2 * DFF], F32, kind="ExternalInput").ap()
        wo = dt_("wo%d" % f, [DFF, D], F32, kind="ExternalInput").ap()
        wi_bf = dt_("wi_bf%d" % f, [D, 2 * DFF], BF16, kind="Internal").ap()
        wo_bf = dt_("wo_bf%d" % f, [DFF, D], BF16, kind="Internal").ap()
        convert_weight(ctx, wi, wi_bf, 128)
        convert_weight(ctx, wo, wo_bf, 256)
        ws.append((g, wi_bf, wo_bf))
    P.barrier()
    scr = [dt_("scr%d" % i, [D, T], F32, kind="Internal").ap() for i in range(2)]
    PSb = [S.ps_h[0][0], S.ps_h[0][1]]
    bPSb = [S.b_ps_h[0][0], S.b_ps_h[0][1]]
    cur, b_cur = x_in, Buf("xin")
    si = 0
    stages = []
    if tin:
        stages.append("tin")
    if cin:
        stages.append("proj")
    stages += ["ffn%d" % f for f in range(n_ffn)]
    for n_i, stg_ in enumerate(stages):
        last = (n_i == len(stages) - 1)
        dst = out if (last and not tout) else scr[si % 2]
        si += 1
        if stg_ == "tin":
            b_n = Buf("tin_out")
            emit_transpose_in(ctx, S, cur, dst, T, PSb, bPSb, b_n)
        elif stg_ == "proj":
            b_n = emit_ffn(ctx, cur, dst, T, None, None, w_o_bf, b_cur, b_w, "proj", pre_act=yT, CC=cin // 128, scale=1.0)
        else:
            g, wi_bf, wo_bf = ws[int(stg_[3:])]
            b_n = emit_ffn(ctx, cur, dst, T, g, wi_bf, wo_bf, b_cur, b_w, stg_)
        P.barrier()
        cur, b_cur = dst, b_n
    fin = ["ffn_st", "ti_st"]
    if tout:
        emit_transpose_out(ctx, S, cur, out, T, PSb, bPSb, b_cur)
        fin = ["to_st"]
    st = P.emit(final_wait_streams=fin)
    return nc, st


DIL_R = (1, 4, 16)
DIL_NC = 5120
KPAD = 1024


def t5_bucket_np(rel):
    n = np.abs(rel)
    far = 8 + (np.log(np.maximum(n, 1).astype(np.float32) / 8) / np.log(1024 / 8) * 8).astype(np.int32)
    far = np.minimum(far, 15)
    return np.where(rel > 0, 16, 0) + np.where(n < 8, n, far)


def host_bias_idx():
    a = np.arange(128)[:, None]
    c = np.arange(128)[None, :]
    idx = np.zeros((3, 2, 128, 128), np.int64)
    for gi, r in enumerate(DIL_R):
        for kt in range(2):
            idx[gi, kt] = t5_bucket_np((a - c - 64 + 128 * kt) * r)
    return idx


def host_dil_masks():
    a = np.arange(128)[:, None]
    c = np.arange(128)[None, :]
    m = np.zeros((128, 4, 128), np.float32)
    m[:, 0] = np.where(a >= c, 0.0, -30000.0)
    m[:, 1] = np.where(a <= c, 0.0, -30000.0)
    m[:, 2] = np.where((a >= c) & (a >= 64), 0.0, -30000.0)
    m[:, 3] = np.where((a <= c) & (a < 64), 0.0, -30000.0)
    return m


def build_dil(S):
    nc = bass.Bass("TRN2", target_bir_lowering=False)
    dt_ = nc.dram_tensor
    xT = dt_("xT", [D, S], F32, kind="ExternalInput").ap()
    gmix = dt_("gmix", [D], F32, kind="ExternalInput").ap()
    wc = dt_("wc", [D, DIL_NC], F32, kind="ExternalInput").ap()
    qkg = dt_("qkg", [128, 6], F32, kind="ExternalInput").ap()
    bias_t = dt_("bias_t", [128, 24, 128], F32, kind="ExternalInput").ap()
    masks = dt_("masks", [128, 4, 128], F32, kind="ExternalInput").ap()
    memT = dt_("memT", [D, 256], F32, kind="ExternalInput").ap()
    gmem = dt_("gmem", [D], F32, kind="ExternalInput").ap()
    wk = dt_("wk", [D, 512], F32, kind="ExternalInput").ap()
    wv = dt_("wv", [D, 512], F32, kind="ExternalInput").ap()
    kg = dt_("kg", [256], F32, kind="ExternalInput").ap()
    qg = dt_("qg", [256], F32, kind="ExternalInput").ap()
    cst = dt_("cst", [128, 7, 128], F32, kind="ExternalInput").ap()
    yT = dt_("yT", [1024, S], BF16, kind="ExternalOutput").ap()
    wc_bf = dt_("wc_bf", [D, DIL_NC], BF16, kind="Internal").ap()
    wk_bf = dt_("wk_bf", [D, 512], BF16, kind="Internal").ap()
    wv_bf = dt_("wv_bf", [D, 512], BF16, kind="Internal").ap()
    qn_d = [dt_("qn_d%d" % i, [512, S], BF16, kind="Internal").ap() for i in range(3)]
    kn_d = [dt_("kn_d%d" % i, [512, S], BF16, kind="Internal").ap() for i in range(3)]
    v_d = [dt_("v_d%d" % i, [S, 512], BF16, kind="Internal").ap() for i in range(3)]

    ctx = Ctx(); ctx.nc = nc; P = ctx.P = Prog(nc)
    mk_consts(ctx)
    load_consts(ctx, cst)
    convert_weight(ctx, wc, wc_bf, 256)
    convert_weight(ctx, wk, wk_bf, 512)
    convert_weight(ctx, wv, wv_bf, 512)
    ctx.b_wcv = Buf("wcv")
    P.barrier()
    PS = [nc.alloc_psum_tensor("ps%d" % i, [128, 512], F32) for i in range(8)]
    bPS = [Buf("ps%d" % i) for i in range(8)]
    N = NormSb(nc, 512, PS[0])
    N.b_ps_ss = bPS[0]
    M = MemSb(nc, 2)
    wbuf = [nc.alloc_sbuf_tensor("wbuf%d" % i, [128, KC, 512], BF16) for i in range(2)]
    b_wbuf = [Buf("wb0"), Buf("wb1")]
    stg = nc.alloc_sbuf_tensor("stg", [128, 4, 512], F32)
    b_stg = Buf("stg")
    stb = nc.alloc_sbuf_tensor("stb", [128, 4, 512], BF16)
    b_stb = Buf("stb")
    qkg_sb = nc.alloc_sbuf_tensor("qkg_sb", [128, 6], F32)
    b_par = Buf("par")
    P.op("sp", lambda: nc.sync.dma_start(out=qkg_sb[:], in_=qkg), writes=[b_par], stream="par")
    b_xin, b_q_d, b_k_d, b_v_d, b_out = Buf("xin"), Buf("q_d"), Buf("k_d"), Buf("v_d"), Buf("out")
    emit_mem_prep(ctx, M, N, memT, gmem, wk_bf, wv_bf, kg, qg, wbuf[0], b_wbuf[0], [PS[1], PS[2]], [bPS[1], bPS[2]])
    TT = 512
    yT_v = yT.rearrange("(j p) t -> p j t", p=128)
    wi = [0]

    def nextw():
        wi[0] += 1
        return wbuf[wi[0] % 2], b_wbuf[wi[0] % 2]

    for t in range(S // TT):
        t0 = t * TT
        emit_norm_tile(ctx, N, xT.rearrange("(k p) t -> p k t", p=128)[:, :, t0:t0 + TT], gmix, TT, b_xin)
        for gi in range(3):
            for qk in range(2):
                wb, bwb = nextw()

                def sink(jj, psap, bps, gi=gi, qk=qk):
                    P.op("act", lambda: nc.scalar.copy(out=stg[:, jj, :], in_=psap[:]), reads=[bps], writes=[b_stg], accum=(jj > 0))
                    P.op("act", lambda: nc.scalar.activation(out=M.sq[0][:], in_=psap[:], func=AF.Square),
                         reads=[bps], writes=[M.b_sq[0]])
                    P.op("pe", lambda: nc.tensor.matmul(PS[3][:], ctx.ones_bf[:], M.sq[0][:], start=True, stop=True),
                         reads=[M.b_sq[0], ctx.b_ones], writes=[bPS[3]])
                    P.op("act", lambda: nc.scalar.activation(out=M.rs[:], in_=PS[3][:], func=AF.Sqrt, bias=ctx.eps_t[:], scale=1.0 / 128),
                         reads=[bPS[3], ctx.b_eps], writes=[M.b_rs])
                    P.op("dve", lambda: nc.vector.reciprocal(out=M.rs[:], in_=M.rs[:]), reads=[M.b_rs], writes=[M.b_rs])
                    P.op("dve", lambda: nc.vector.scalar_tensor_tensor(
                        out=stb[:, jj, :], in0=stg[:, jj, :], scalar=qkg_sb[:, gi * 2 + qk:gi * 2 + qk + 1], in1=M.rs[:],
                        op0=ALU.mult, op1=ALU.mult), reads=[b_stg, b_par, M.b_rs], writes=[b_stb], accum=(jj > 0))
                emit_proj_fm(ctx, N, wb, bwb, wc_bf, (gi * 3 + qk) * 512, 512, [PS[1], PS[2]], [bPS[1], bPS[2]], TT, sink)
                dd_ = (qn_d if qk == 0 else kn_d)[gi]
                P.op("sp", lambda dd_=dd_, t0=t0: nc.sync.dma_start(
                    out=dd_.rearrange("(j p) t -> p j t", p=128)[:, :, t0:t0 + TT], in_=stb[:]),
                    reads=[b_stb], writes=[b_q_d if qk == 0 else b_k_d], stream="st_qk", accum=True)
            wb, bwb = nextw()

            def vsink(i, psap, bps):
                P.op("act", lambda: nc.scalar.copy(out=stb[:, i, :], in_=psap[:]), reads=[bps], writes=[b_stb], accum=(i > 0))
            emit_proj_tm(ctx, N, wb, bwb, wc_bf, (gi * 3 + 2) * 512, 512, [PS[1], PS[2]], [bPS[1], bPS[2]], TT, vsink)
            P.op("sp", lambda gi=gi, t0=t0: nc.sync.dma_start(
                out=v_d[gi].rearrange("(i p) f -> p i f", p=128)[:, t0 // 128:t0 // 128 + 4, :], in_=stb[:]),
                reads=[b_stb], writes=[b_v_d], stream="st_v", accum=True)
        wb, bwb = nextw()

        def qsink(jj, psap, bps):
            P.op("act", lambda: nc.scalar.copy(out=M.qT[:, jj, :], in_=psap[:]), reads=[bps], writes=[M.b_qT], accum=(jj > 0))
        emit_proj_fm(ctx, N, wb, bwb, wc_bf, 9 * 512, 512, [PS[1], PS[2]], [bPS[1], bPS[2]], TT, qsink)
        emit_mem_attn_tile(ctx, M, [PS[4], PS[5]], [bPS[4], bPS[5]],
                           lambda hh, dc, t0=t0: yT_v[:, 4 + hh * 2 + dc, t0:t0 + TT], b_out, "st_mo")
    P.barrier()

    sb = nc.alloc_sbuf_tensor
    BLK = 2048
    KT = [sb("KT%d" % i, [128, KPAD + S + KPAD], BF16) for i in range(3)]
    b_KT = [Buf("KT%d" % i) for i in range(3)]
    qb = [N.x[:, 2 * i:2 * i + 2, :].rearrange("p k t -> p (k t)").bitcast(BF16)[:, 0:BLK] for i in range(3)]
    b_qb = [Buf("qb%d" % i) for i in range(3)]
    onum = N.x[:, 8:12, :].rearrange("p k t -> p (k t)")
    oden = N.x[:, 12:16, :].rearrange("p k t -> p (k t)")
    b_acc = Buf("oacc")
    bm = [wbuf[0][:, 0:12, :].rearrange("p k t -> p (k t)").bitcast(F32).rearrange("p (n c) -> p n c", c=128),
          wbuf[1][:, 0:12, :].rearrange("p k t -> p (k t)").bitcast(F32).rearrange("p (n c) -> p n c", c=128)]
    msk = sb("msk", [128, 4, 128], F32)
    b_bm = Buf("bm")
    P.op("sp", lambda: nc.sync.dma_start(out=msk[:], in_=masks), writes=[b_bm], stream="bmld")
    P.op("sp", lambda: nc.sync.dma_start(out=bm[0], in_=bias_t), writes=[b_bm], stream="bmld", accum=True)
    for n in range(24):
        kt = n % 2
        P.op("dve", lambda n=n, kt=kt: nc.vector.tensor_tensor(out=bm[1][:, n, :], in0=bm[0][:, n, :], in1=msk[:, 2 + kt, :], op=ALU.add),
             reads=[b_bm], writes=[b_bm], accum=True)
    for n in range(24):
        kt = n % 2
        P.op("dve", lambda n=n, kt=kt: nc.vector.tensor_tensor(out=bm[0][:, n, :], in0=bm[0][:, n, :], in1=msk[:, kt, :], op=ALU.add),
             reads=[b_bm], writes=[b_bm], accum=True)
    for gi in range(3):
        P.op("pool", lambda gi=gi: nc.gpsimd.memset(KT[gi][:, 0:KPAD], 0.0), writes=[b_KT[gi]])
        P.op("pool", lambda gi=gi: nc.gpsimd.memset(KT[gi][:, KPAD + S:], 0.0), writes=[b_KT[gi]], accum=True)
    sbt = [sb("sbt%d" % i, [128, 128], F32) for i in range(2)]
    pT = [sb("dpT%d" % i, [128, 128], BF16) for i in range(2)]
    Vt = [sb("Vt%d" % i, [128, 128], BF16) for i in range(2)]
    ob = sb("ob", [128, BLK], BF16)
    rd = sb("rd", [128, BLK], F32)
    b_sbt, b_pT, b_Vt = [Buf("sbt0"), Buf("sbt1")], [Buf("dpT0"), Buf("dpT1")], [Buf("Vt0"), Buf("Vt1")]
    b_ob, b_rd = Buf("ob"), Buf("rd")
    ps_s, b_s = [PS[1], PS[2]], [bPS[1], bPS[2]]
    ps_o, b_o = PS[3], bPS[3]
    ps_d, b_d = PS[4], bPS[4]
    scale = 128 ** -0.5
    for h in range(4):
        for gi in range(3):
            P.op("sp", lambda gi=gi, h=h: nc.sync.dma_start(out=KT[gi][:, KPAD:KPAD + S], in_=kn_d[gi][h * 128:(h + 1) * 128, :]),
                 reads=[b_k_d], writes=[b_KT[gi]], stream="ld_K%d" % gi, accum=True)
        for blk in range(S // BLK):
            tb = blk * BLK
            P.op("pool", lambda: nc.gpsimd.memset(onum, 0.0), writes=[b_acc])
            P.op("pool", lambda: nc.gpsimd.memset(oden, 0.0), writes=[b_acc], accum=True)
            for gi, r in enumerate(DIL_R):
                L = S // r
                P.op("sp", lambda gi=gi, h=h, tb=tb: nc.sync.dma_start(out=qb[gi], in_=qn_d[gi][h * 128:(h + 1) * 128, tb:tb + BLK]),
                     reads=[b_q_d], writes=[b_qb[gi]], stream="ld_q%d" % gi)
                span = 128 * r
                for sbk in range(BLK // span):
                    for m in range(r):
                        j0 = (tb + span * sbk) // r
                        qsl = slice(span * sbk + m, span * sbk + m + 127 * r + 1, r)
                        for kt in range(2):
                            tok0 = tb + span * sbk + (128 * kt - 64) * r + m
                            first = (j0 == 0 and kt == 0)
                            lastt = (j0 + 128 == L and kt == 1)
                            n_b = (gi * 4 + h) * 2 + kt
                            bmt = bm[1][:, n_b, :] if (first or lastt) else bm[0][:, n_b, :]
                            ksl = slice(KPAD + tok0, KPAD + tok0 + 127 * r + 1, r)
                            P.op("pe", lambda gi=gi, ksl=ksl, qsl=qsl, kt=kt: nc.tensor.matmul(
                                ps_s[kt][:, 0:128], KT[gi][:, ksl], qb[gi][:, qsl], start=True, stop=True),
                                reads=[b_KT[gi], b_qb[gi]], writes=[b_s[kt]])
                            P.op("dve", lambda kt=kt, bmt=bmt: nc.vector.scalar_tensor_tensor(
                                out=sbt[kt][:], in0=ps_s[kt][:, 0:128], scalar=scale, in1=bmt, op0=ALU.mult, op1=ALU.add),
                                reads=[b_s[kt], b_bm], writes=[b_sbt[kt]])
                            P.op("act", lambda kt=kt: nc.scalar.activation(out=pT[kt][:], in_=sbt[kt][:], func=AF.Exp),
                                 reads=[b_sbt[kt]], writes=[b_pT[kt]])
                            a_lo, a_hi = (64, 128) if first else ((0, 64) if lastt else (0, 128))
                            if first or lastt:
                                P.op("pool", lambda kt=kt: nc.gpsimd.memset(Vt[kt][:], 0.0), writes=[b_Vt[kt]])
                            r_lo = tok0 + a_lo * r
                            P.op("sp", lambda gi=gi, kt=kt, r_lo=r_lo, a_lo=a_lo, a_hi=a_hi, r=r, h=h: nc.sync.dma_start(
                                out=Vt[kt][a_lo:a_hi, :],
                                in_=v_d[gi][r_lo:r_lo + (a_hi - a_lo - 1) * r + 1:r, h * 128:(h + 1) * 128]),
                                reads=[b_v_d], writes=[b_Vt[kt]], stream="ld_V%d" % kt, accum=(first or lastt))
                            P.op("pe", lambda kt=kt: nc.tensor.matmul(ps_o[:, 0:128], Vt[kt][:], pT[kt][:], start=(kt == 0), stop=(kt == 1)),
                                 reads=[b_Vt[kt], b_pT[kt]], writes=[b_o], accum=(kt > 0))
                            P.op("pe", lambda kt=kt: nc.tensor.matmul(ps_d[:, 0:128], ctx.ones_bf[:], pT[kt][:], start=(kt == 0), stop=(kt == 1)),
                                 reads=[ctx.b_ones, b_pT[kt]], writes=[b_d], accum=(kt > 0))
                        P.op("dve", lambda qsl=qsl: nc.vector.tensor_tensor(out=onum[:, qsl], in0=onum[:, qsl], in1=ps_o[:, 0:128], op=ALU.add),
                             reads=[b_o, b_acc], writes=[b_acc], accum=True)
                        P.op("dve", lambda qsl=qsl: nc.vector.tensor_tensor(out=oden[:, qsl], in0=oden[:, qsl], in1=ps_d[:, 0:128], op=ALU.add),
                             reads=[b_d, b_acc], writes=[b_acc], accum=True)
            P.op("dve", lambda: nc.vector.reciprocal(out=rd[:], in_=oden), reads=[b_acc], writes=[b_rd])
            P.op("dve", lambda: nc.vector.tensor_tensor(out=ob[:], in0=onum, in1=rd[:], op=ALU.mult),
                 reads=[b_acc, b_rd], writes=[b_ob])
            P.op("sp", lambda h=h, tb=tb: nc.sync.dma_start(out=yT[h * 128:(h + 1) * 128, tb:tb + BLK], in_=ob[:]),
                 reads=[b_ob], writes=[b_out], stream="st_o", accum=True)
    st = P.emit(final_wait_streams=["st_o", "st_mo0", "st_mo1"])
    return nc, st


def ssd_args(inp, i, hh, x1T, memT):
    j = i // 2
    w = inp["ssd_w_in"][j]
    ar = np.arange
    cols = np.concatenate([ar(hh * 1536, (hh + 1) * 1536), 3072 + ar(hh * 1536, (hh + 1) * 1536),
                           6144 + ar(hh * 512, (hh + 1) * 512), 7168 + ar(hh * 512, (hh + 1) * 512),
                           8192 + ar(hh * 24, (hh + 1) * 24), 8240 + ar(hh * 24, (hh + 1) * 24),
                           8288 + ar(hh * 512, (hh + 1) * 512)])
    cch = np.concatenate([ar(hh * 1536, (hh + 1) * 1536), 3072 + ar(hh * 512, (hh + 1) * 512),
                          4096 + ar(hh * 512, (hh + 1) * 512)])
    sl = slice(hh * 24, (hh + 1) * 24)
    c_ = np.ascontiguousarray
    return {
        "xT": x1T, "gmix": c_(inp["mix_norm"][i]), "wc": c_(w[:, cols]),
        "conv_w": c_(inp["ssd_conv_w"][j][:, cch].T), "conv_b": c_(inp["ssd_conv_b"][j][cch]),
        "dt_bias": c_(inp["ssd_dt_bias"][j][:, sl].reshape(1, 48)),
        "a_log": c_(inp["ssd_A_log"][j][:, sl].reshape(1, 48)),
        "dskip": c_(inp["ssd_D"][j][sl].reshape(1, 24)),
        "norm_g": c_(inp["ssd_norm"][j][hh * 1536:(hh + 1) * 1536].reshape(1, 1536)),
        "memT": memT, "gmem": c_(inp["mem_norm"][i]),
        "wk": c_(inp["mem_w_kv"][i][:, hh * 512:(hh + 1) * 512]),
        "wv": c_(inp["mem_w_kv"][i][:, 1024 + hh * 512:1024 + (hh + 1) * 512]),
        "kg": c_(inp["mem_k_gain"][i]), "qg": c_(inp["mem_q_gain"][i]), "cst": host_consts(),
    }


def dil_args(inp, i, hh, xT, memT):
    j = i // 2
    w = inp["dil_w_in"][j]
    ar = np.arange
    cols = []
    for gi in range(3):
        for which in range(3):
            cols.append(gi * 3072 + which * 1024 + ar(hh * 512, (hh + 1) * 512))
    cols.append(9216 + ar(hh * 512, (hh + 1) * 512))
    cols = np.concatenate(cols)
    c_ = np.ascontiguousarray
    qkg = np.stack([inp["dil_q_gain"][j][0], inp["dil_k_gain"][j][0], inp["dil_q_gain"][j][1], inp["dil_k_gain"][j][1],
                    inp["dil_q_gain"][j][2], inp["dil_k_gain"][j][2]], axis=1)
    idx = host_bias_idx()
    rb = inp["rel_bias"]
    bt = np.zeros((128, 24, 128), np.float32)
    for gi in range(3):
        for h in range(4):
            for kt in range(2):
                bt[:, (gi * 4 + h) * 2 + kt, :] = rb[idx[gi, kt], gi * 8 + hh * 4 + h]
    return {
        "xT": xT, "gmix": c_(inp["mix_norm"][i]), "wc": c_(w[:, cols]), "qkg": c_(qkg.astype(np.float32)),
        "bias_t": bt, "masks": host_dil_masks(),
        "memT": memT, "gmem": c_(inp["mem_norm"][i]),
        "wk": c_(inp["mem_w_kv"][i][:, hh * 512:(hh + 1) * 512]),
        "wv": c_(inp["mem_w_kv"][i][:, 1024 + hh * 512:1024 + (hh + 1) * 512]),
        "kg": c_(inp["mem_k_gain"][i]), "qg": c_(inp["mem_q_gain"][i]), "cst": host_consts(),
    }


_CACHE = {}


def _get(name, fn):
    if name not in _CACHE:
        _CACHE[name] = fn()[0]
    return _CACHE[name]


def kernel(**inputs):
    inp = {k: np.asarray(v) for k, v in inputs.items()}
    x = inp["x"]
    Bn, S, _ = x.shape
    T = S // 2
    NCORE = Bn * 2
    cores = list(range(NCORE))
    c_ = np.ascontiguousarray
    cst = host_consts()
    memT = [c_(inp["mem"][b].T) for b in range(Bn)]

    def ffn_w(f, i, k):
        return {"g%d" % f: c_(inp["ffn_norm"][i, k]), "wi%d" % f: c_(inp["ffn_w_in"][i, k]), "wo%d" % f: c_(inp["ffn_w_out"][i, k])}

    nc1 = _get("tok1", lambda: build_tok(T, 1, True, False, 0))
    maps = []
    for c in cores:
        b, hf = divmod(c, 2)
        m = {"x": c_(x[b, hf * T:(hf + 1) * T, :]), "cst": cst}
        m.update(ffn_w(0, 0, 0))
        maps.append(m)
    r1 = run_bass_kernel_spmd(nc1, maps, core_ids=cores).results
    x1T = [r["outT"] for r in r1]
    nc2 = _get("ssd", lambda: build_ssd(S))
    maps = []
    for c in cores:
        b, hh = divmod(c, 2)
        full = np.concatenate([x1T[2 * b], x1T[2 * b + 1]], axis=1)
        maps.append(ssd_args(inp, 0, hh, c_(full), memT[b]))
    r2 = run_bass_kernel_spmd(nc2, maps, core_ids=cores).results
    y2 = [r["yT"] for r in r2]
    del maps
    nc3 = _get("tok3", lambda: build_tok(T, 2, False, False, 4096))
    maps = []
    for c in cores:
        b, hf = divmod(c, 2)
        ts = slice(hf * T, (hf + 1) * T)
        ya, yb = y2[2 * b], y2[2 * b + 1]
        yin = np.concatenate([ya[0:1536, ts], yb[0:1536, ts], ya[1536:, ts], yb[1536:, ts]], axis=0)
        m = {"xT": x1T[c], "cst": cst, "yT": c_(yin), "w_o": c_(inp["ssd_w_out"][0])}
        m.update(ffn_w(0, 0, 1))
        m.update(ffn_w(1, 1, 0))
        maps.append(m)
    r3 = run_bass_kernel_spmd(nc3, maps, core_ids=cores).results
    x4T = [r["outT"] for r in r3]
    del maps, x1T, y2
    nc4 = _get("dil", lambda: build_dil(S))
    maps = []
    for c in cores:
        b, hh = divmod(c, 2)
        full = np.concatenate([x4T[2 * b], x4T[2 * b + 1]], axis=1)
        maps.append(dil_args(inp, 1, hh, c_(full), memT[b]))
    r4 = run_bass_kernel_spmd(nc4, maps, core_ids=cores).results
    y4 = [r["yT"] for r in r4]
    del maps
    nc5 = _get("tok5", lambda: build_tok(T, 1, False, True, 2048))
    maps = []
    for c in cores:
        b, hf = divmod(c, 2)
        ts = slice(hf * T, (hf + 1) * T)
        ya, yb = y4[2 * b], y4[2 * b + 1]
        yin = np.concatenate([ya[0:512, ts], yb[0:512, ts], ya[512:, ts], yb[512:, ts]], axis=0)
        m = {"xT": x4T[c], "cst": cst, "yT": c_(yin), "w_o": c_(inp["dil_w_out"][0])}
        m.update(ffn_w(0, 1, 1))
        maps.append(m)
    r5 = run_bass_kernel_spmd(nc5, maps, core_ids=cores).results
    out = np.zeros((Bn, S, D), np.float32)
    for c in cores:
        b, hf = divmod(c, 2)
        out[b, hf * T:(hf + 1) * T, :] = r5[c]["out"]
    return out
```

```python
import numpy as np
import concourse.bass as bass
import concourse.mybir as mybir
from concourse.bass_utils import run_bass_kernel_spmd

F32 = mybir.dt.float32
BF16 = mybir.dt.bfloat16
AF = mybir.ActivationFunctionType
ALU = mybir.AluOpType
AX = mybir.AxisListType

D = 2048
KC = D // 128
DFF = 5632
JC = DFF // 128
EPS = 1e-6
SAME_ENGINE_SYNC = True


class Buf:
    __slots__ = ("name", "w", "r")

    def __init__(self, name):
        self.name = name
        self.w = []
        self.r = []


class Op:
    __slots__ = ("eng", "fn", "deps", "stream", "is_dma", "signaled", "count")

    def __init__(self, eng, fn, deps, stream=None):
        self.eng = eng
        self.fn = fn
        self.deps = deps
        self.stream = stream
        self.is_dma = stream is not None
        self.signaled = False
        self.count = 0


class Prog:
    ENGS = ("pe", "dve", "act", "pool", "sp")

    def __init__(self, nc):
        self.nc = nc
        self.ops = []
        self.same_engine_sync = SAME_ENGINE_SYNC
        self.bar = set()
        self.last = {}
        self.e = {"pe": nc.tensor, "dve": nc.vector, "act": nc.scalar, "pool": nc.gpsimd, "sp": nc.sync}

    def op(self, eng, fn, reads=(), writes=(), stream=None, accum=False):
        idx = len(self.ops)
        deps = set()
        for b in reads:
            deps.update(b.w)
        for b in writes:
            deps.update(b.w)
            deps.update(b.r)
        deps.discard(idx)
        deps.update(self.bar)
        red = {}
        for d in deps:
            p = self.ops[d]
            key = ("dma", p.stream) if p.is_dma else ("eng", p.eng)
            if d > red.get(key, -1):
                red[key] = d
        deps = set(red.values())
        self.ops.append(Op(eng, fn, deps, stream))
        self.last[("dma", stream) if stream is not None else ("eng", eng)] = idx
        for b in reads:
            b.r.append(idx)
        for b in writes:
            if accum:
                b.w.append(idx)
            else:
                b.w = [idx]
                b.r = []
        return idx

    def barrier(self):
        self.bar = set(self.last.values())

    def emit(self, final_wait_streams=()):
        nc = self.nc
        ops = self.ops
        for i, o in enumerate(ops):
            if o.is_dma:
                o.signaled = True
                continue
        for i, o in enumerate(ops):
            for d in o.deps:
                p = ops[d]
                if p.is_dma:
                    continue
                if p.eng != o.eng or (self.same_engine_sync and o.eng != "pe") or o.is_dma:
                    p.signaled = True
        cnt = {}
        sems = {}
        for o in ops:
            key = ("dma", o.stream) if o.is_dma else ("eng", o.eng)
            if o.signaled:
                cnt[key] = cnt.get(key, 0) + (16 if o.is_dma else 1)
                o.count = cnt[key]
                if key not in sems:
                    sems[key] = nc.alloc_semaphore(name="s_%s_%s" % key)
            else:
                o.count = None
        seen = {e: {} for e in self.ENGS}
        n_wait = 0
        for i, o in enumerate(ops):
            eng = self.e[o.eng]
            need = {}
            for d in o.deps:
                p = ops[d]
                key = ("dma", p.stream) if p.is_dma else ("eng", p.eng)
                if not p.is_dma and p.eng == o.eng and not ((self.same_engine_sync and o.eng != "pe") or o.is_dma):
                    continue
                assert p.signaled
                if p.count > need.get(key, 0):
                    need[key] = p.count
            for key, c in need.items():
                if c > seen[o.eng].get(key, 0):
                    eng.wait_ge(sems[key], c)
                    seen[o.eng][key] = c
                    n_wait += 1
            ins = o.fn()
            if o.signaled:
                key = ("dma", o.stream) if o.is_dma else ("eng", o.eng)
                ins.then_inc(sems[key], 16 if o.is_dma else 1)
        for s in final_wait_streams:
            key = ("dma", s)
            if key in cnt:
                nc.sync.wait_ge(sems[key], cnt[key])
        self.stats = dict(n_ops=len(ops), n_wait=n_wait, n_sems=len(sems),
                          max_count=max(cnt.values()) if cnt else 0)
        return self.stats


class Ctx:
    pass


def mk_consts(ctx):
    nc, P = ctx.nc, ctx.P
    ctx.ones_bf = nc.alloc_sbuf_tensor("ones_bf", [128, 128], BF16)
    ctx.b_ones = Buf("ones")
    P.op("pool", lambda: nc.gpsimd.memset(ctx.ones_bf[:], 1.0), writes=[ctx.b_ones])
    ctx.eps_t = nc.alloc_sbuf_tensor("eps_t", [128, 1], F32)
    ctx.b_eps = Buf("eps")
    P.op("pool", lambda: nc.gpsimd.memset(ctx.eps_t[:], EPS), writes=[ctx.b_eps])


def convert_weight(ctx, src_ap, dst_ap, rows_per_dma):
    nc, P = ctx.nc, ctx.P
    R = src_ap.shape[0]
    b = Buf("cv")
    for r0 in range(0, R, rows_per_dma):
        r1 = min(R, r0 + rows_per_dma)
        P.op("pool", (lambda a=dst_ap[r0:r1], s=src_ap[r0:r1]: nc.gpsimd.dma_start(out=a, in_=s)),
             writes=[b], stream="cvt", accum=True)
    return b


def emit_ffn(ctx, x_in, x_out, T, g_ap, w_in_bf, w_out_bf, b_x_in, b_wcv, tag, pre_act=None, CC=JC, scale=0.5):
    nc, P = ctx.nc, ctx.P
    TT = 512
    assert T % TT == 0
    FW = 256
    NG = DFF // FW
    DW = 256
    NM = D // DW
    S = ctx.ffn_sb
    b_out = Buf("ffn_out_" + tag)
    for t in range(T // TT):
        t0 = t * TT
        if pre_act is not None:
            P.op("sp", lambda t0=t0: nc.sync.dma_start(
                out=S.x[:], in_=x_in.rearrange("(k p) t -> p k t", p=128)[:, :, t0:t0 + TT]),
                reads=[b_x_in], writes=[S.b_x], stream="ffn_x")
            P.op("sp", lambda t0=t0: nc.sync.dma_start(
                out=S.act[:, 0:CC, :], in_=pre_act.rearrange("(j p) t -> p j t", p=128)[:, :, t0:t0 + TT]),
                reads=[b_x_in], writes=[S.b_act], stream="ffn_pa")
        else:
            P.op("sp", lambda: nc.sync.dma_start(out=S.g[:], in_=g_ap.rearrange("(k p) -> p k", p=128),
                                                 allow_slow_non_contiguous=True),
                 reads=[b_wcv], writes=[S.b_g], stream="ffn_g")
            P.op("sp", lambda t0=t0: nc.sync.dma_start(
                out=S.x[:], in_=x_in.rearrange("(k p) t -> p k t", p=128)[:, :, t0:t0 + TT]),
                reads=[b_x_in], writes=[S.b_x], stream="ffn_x")
            for k in range(KC):
                sq = S.sq[k % 2]
                P.op("act", lambda k=k, sq=sq: nc.scalar.activation(out=sq[:], in_=S.x[:, k, :], func=AF.Square),
                     reads=[S.b_x], writes=[S.b_sq[k % 2]])
                P.op("pe", lambda k=k, sq=sq: nc.tensor.matmul(S.ps_ss[:], ctx.ones_bf[:], sq[:],
                                                             start=(k == 0), stop=(k == KC - 1)),
                     reads=[S.b_sq[k % 2], ctx.b_ones], writes=[S.b_ps_ss], accum=(k > 0))
            P.op("act", lambda: nc.scalar.activation(out=S.rstd[:], in_=S.ps_ss[:], func=AF.Sqrt,
                                                     bias=ctx.eps_t[:], scale=1.0 / D),
                 reads=[S.b_ps_ss, ctx.b_eps], writes=[S.b_rstd])
            P.op("dve", lambda: nc.vector.reciprocal(out=S.rstd[:], in_=S.rstd[:]),
                 reads=[S.b_rstd], writes=[S.b_rstd])
            for k in range(KC):
                eng = "dve"
                P.op(eng, lambda k=k, eng=eng: ctx.P.e[eng].scalar_tensor_tensor(
                    out=S.xn[:, k, :], in0=S.x[:, k, :], scalar=S.g[:, k:k + 1], in1=S.rstd[:],
                    op0=ALU.mult, op1=ALU.mult),
                    reads=[S.b_x, S.b_g, S.b_rstd], writes=[S.b_xn], accum=(k > 0))
            for gi in range(NG):
                wb = gi % 2
                f0 = gi * FW
                P.op("sp", lambda wb=wb, f0=f0: nc.sync.dma_start(
                    out=S.win[wb][:, 0, :, :], in_=w_in_bf.rearrange("(k p) f -> p k f", p=128)[:, :, f0:f0 + FW]),
                    reads=[b_wcv], writes=[S.b_win[wb]], stream="ffn_win%d" % wb)
                P.op("sp", lambda wb=wb, f0=f0: nc.sync.dma_start(
                    out=S.win[wb][:, 1, :, :],
                    in_=w_in_bf.rearrange("(k p) f -> p k f", p=128)[:, :, DFF + f0:DFF + f0 + FW]),
                    reads=[b_wcv], writes=[S.b_win[wb]], stream="ffn_win%d" % wb, accum=True)
                for jj in range(FW // 128):
                    j = gi * (FW // 128) + jj
                    pb = j % 2
                    for gu in range(2):
                        ps = S.ps_h[pb][gu]
                        for k in range(KC):
                            P.op("pe", lambda wb=wb, gu=gu, k=k, jj=jj, ps=ps: nc.tensor.matmul(
                                ps[:], S.win[wb][:, gu, k, jj * 128:(jj + 1) * 128], S.xn[:, k, :],
                                start=(k == 0), stop=(k == KC - 1)),
                                reads=[S.b_win[wb], S.b_xn], writes=[S.b_ps_h[pb][gu]], accum=(k > 0))
                    P.op("act", lambda pb=pb: nc.scalar.activation(out=S.silu[pb][:], in_=S.ps_h[pb][0][:], func=AF.Silu),
                         reads=[S.b_ps_h[pb][0]], writes=[S.b_silu[pb]])
                    P.op("dve", lambda pb=pb, j=j: nc.vector.tensor_tensor(
                        out=S.act[:, j, :], in0=S.silu[pb][:], in1=S.ps_h[pb][1][:], op=ALU.mult),
                        reads=[S.b_silu[pb], S.b_ps_h[pb][1]], writes=[S.b_act], accum=(j > 0))
        for mi in range(NM):
            wb = mi % 2
            d0 = mi * DW
            half = CC // 2
            P.op("sp", lambda wb=wb, d0=d0: nc.sync.dma_start(
                out=S.wout[wb][:, 0:half, :],
                in_=w_out_bf.rearrange("(j p) d -> p j d", p=128)[:, 0:half, d0:d0 + DW]),
                reads=[b_wcv], writes=[S.b_wout[wb]], stream="ffn_wout%d" % wb)
            P.op("sp", lambda wb=wb, d0=d0: nc.sync.dma_start(
                out=S.wout[wb][:, half:CC, :],
                in_=w_out_bf.rearrange("(j p) d -> p j d", p=128)[:, half:CC, d0:d0 + DW]),
                reads=[b_wcv], writes=[S.b_wout[wb]], stream="ffn_wout%d" % wb, accum=True)
            for mm in range(DW // 128):
                m = mi * (DW // 128) + mm
                pb = m % 2
                ps = S.ps_o[pb]
                for j in range(CC):
                    P.op("pe", lambda wb=wb, j=j, mm=mm, ps=ps: nc.tensor.matmul(
                        ps[:], S.wout[wb][:, j, mm * 128:(mm + 1) * 128], S.act[:, j, :],
                        start=(j == 0), stop=(j == CC - 1)),
                        reads=[S.b_wout[wb], S.b_act], writes=[S.b_ps_o[pb]], accum=(j > 0))
                P.op("dve", lambda m=m, ps=ps: nc.vector.scalar_tensor_tensor(
                    out=S.x[:, m, :], in0=ps[:], scalar=scale, in1=S.x[:, m, :], op0=ALU.mult, op1=ALU.add),
                    reads=[S.b_ps_o[pb], S.b_x], writes=[S.b_x], accum=True)
        P.op("sp", lambda t0=t0: nc.sync.dma_start(
            out=x_out.rearrange("(k p) t -> p k t", p=128)[:, :, t0:t0 + TT], in_=S.x[:]),
            reads=[S.b_x], writes=[b_out], stream="ffn_st", accum=True)
    return b_out


class FfnSb:
    def __init__(self, nc):
        TT = 512
        self.x = nc.alloc_sbuf_tensor("f_x", [128, KC, TT], F32)
        self.xn = nc.alloc_sbuf_tensor("f_xn", [128, KC, TT], BF16)
        self.act = nc.alloc_sbuf_tensor("f_act", [128, JC, TT], BF16)
        self.win = [nc.alloc_sbuf_tensor("f_win%d" % i, [128, 2, KC, 256], BF16) for i in range(2)]
        self.wout = [nc.alloc_sbuf_tensor("f_wout%d" % i, [128, JC, 256], BF16) for i in range(2)]
        self.sq = [nc.alloc_sbuf_tensor("f_sq%d" % i, [128, TT], BF16) for i in range(2)]
        self.silu = [nc.alloc_sbuf_tensor("f_silu%d" % i, [128, TT], F32) for i in range(2)]
        self.rstd = nc.alloc_sbuf_tensor("f_rstd", [128, TT], F32)
        self.g = nc.alloc_sbuf_tensor("f_g", [128, KC], F32)
        self.ps_ss = nc.alloc_psum_tensor("p_ss", [128, TT], F32)
        self.ps_h = [[nc.alloc_psum_tensor("p_h%d%d" % (i, j), [128, TT], F32) for j in range(2)] for i in range(2)]
        self.ps_o = [nc.alloc_psum_tensor("p_o%d" % i, [128, TT], F32) for i in range(2)]
        B = Buf
        self.b_x, self.b_xn, self.b_act = B("x"), B("xn"), B("act")
        self.b_win = [B("win0"), B("win1")]
        self.b_wout = [B("wout0"), B("wout1")]
        self.b_sq = [B("sq0"), B("sq1")]
        self.b_silu = [B("silu0"), B("silu1")]
        self.b_rstd, self.b_g, self.b_ps_ss = B("rstd"), B("g"), B("ps_ss")
        self.b_ps_h = [[B("ph00"), B("ph01")], [B("ph10"), B("ph11")]]
        self.b_ps_o = [B("po0"), B("po1")]


def host_consts():
    i = np.arange(128)
    u, t = i[:, None], i[None, :]
    c = np.zeros((128, 7, 128), np.float32)
    c[:, 0] = (u == t)
    c[:, 1] = (u <= t)
    c[:, 2] = (u >= t)
    c[:, 3] = (u > t)
    c[:, 4] = (u < t)
    c[:, 5] = -30000.0 * (t < u)
    c[:, 6] = -30000.0 * (t > u)
    return c


def load_consts(ctx, c_ap):
    nc, P = ctx.nc, ctx.P
    ctx.cf = nc.alloc_sbuf_tensor("cf", [128, 7, 128], F32)
    ctx.ident_bf = nc.alloc_sbuf_tensor("ident_bf", [128, 128], BF16)
    ctx.b_cf = Buf("cf")
    P.op("sp", lambda: nc.sync.dma_start(out=ctx.cf[:], in_=c_ap), writes=[ctx.b_cf], stream="cst")
    P.op("dve", lambda: nc.vector.tensor_copy(out=ctx.ident_bf[:], in_=ctx.cf[:, 0, :]),
         reads=[ctx.b_cf], writes=[ctx.b_cf], accum=True)


def emit_norm_tile(ctx, S, x_src_ap, g_ap, TT, b_src, eps_scale=None):
    nc, P = ctx.nc, ctx.P
    P.op("sp", lambda: nc.sync.dma_start(out=S.g[:], in_=g_ap.rearrange("(k p) -> p k", p=128),
                                         allow_slow_non_contiguous=True),
         writes=[S.b_g], stream="n_g")
    P.op("sp", lambda: nc.sync.dma_start(out=S.x[:, :, 0:TT], in_=x_src_ap), reads=[b_src], writes=[S.b_x], stream="n_x")
    for k in range(KC):
        sq = S.sq[k % 2]
        P.op("act", lambda k=k, sq=sq: nc.scalar.activation(out=sq[:, 0:TT], in_=S.x[:, k, 0:TT], func=AF.Square),
             reads=[S.b_x], writes=[S.b_sq[k % 2]])
        P.op("pe", lambda k=k, sq=sq: nc.tensor.matmul(S.ps_ss[:, 0:TT], ctx.ones_bf[:], sq[:, 0:TT],
                                                     start=(k == 0), stop=(k == KC - 1)),
             reads=[S.b_sq[k % 2], ctx.b_ones], writes=[S.b_ps_ss], accum=(k > 0))
    P.op("act", lambda: nc.scalar.activation(out=S.rstd[:, 0:TT], in_=S.ps_ss[:, 0:TT], func=AF.Sqrt,
                                             bias=ctx.eps_t[:], scale=1.0 / D),
         reads=[S.b_ps_ss, ctx.b_eps], writes=[S.b_rstd])
    P.op("dve", lambda: nc.vector.reciprocal(out=S.rstd[:, 0:TT], in_=S.rstd[:, 0:TT]),
         reads=[S.b_rstd], writes=[S.b_rstd])
    for k in range(KC):
        P.op("dve", lambda k=k: nc.vector.scalar_tensor_tensor(
            out=S.xn[:, k, 0:TT], in0=S.x[:, k, 0:TT], scalar=S.g[:, k:k + 1], in1=S.rstd[:, 0:TT],
            op0=ALU.mult, op1=ALU.mult),
            reads=[S.b_x, S.b_g, S.b_rstd], writes=[S.b_xn], accum=(k > 0))


class NormSb:
    def __init__(self, nc, TT, ps_ss):
        self.x = nc.alloc_sbuf_tensor("n_x", [128, KC, TT], F32)
        self.xn = nc.alloc_sbuf_tensor("n_xn", [128, KC, TT], BF16)
        self.sq = [nc.alloc_sbuf_tensor("n_sq%d" % i, [128, TT], BF16) for i in range(2)]
        self.rstd = nc.alloc_sbuf_tensor("n_rstd", [128, TT], F32)
        self.g = nc.alloc_sbuf_tensor("n_g", [128, KC], F32)
        self.ps_ss = ps_ss
        B = Buf
        self.b_x, self.b_xn, self.b_rstd, self.b_g, self.b_ps_ss = B("x"), B("xn"), B("rstd"), B("g"), B("pss")
        self.b_sq = [B("sq0"), B("sq1")]


def emit_proj_fm(ctx, S, wbuf, b_wbuf, w_bf, c0, ncols, ps, b_ps, TT, sink):
    nc, P = ctx.nc, ctx.P
    P.op("sp", lambda: nc.sync.dma_start(
        out=wbuf[:, :, 0:ncols], in_=w_bf.rearrange("(k p) f -> p k f", p=128)[:, :, c0:c0 + ncols]),
        reads=[ctx.b_wcv], writes=[b_wbuf], stream="wst_" + b_wbuf.name)
    for jj in range(ncols // 128):
        pi = jj % len(ps)
        for k in range(KC):
            P.op("pe", lambda k=k, jj=jj, pi=pi: nc.tensor.matmul(
                ps[pi][:, 0:TT], wbuf[:, k, jj * 128:(jj + 1) * 128], S.xn[:, k, 0:TT],
                start=(k == 0), stop=(k == KC - 1)),
                reads=[b_wbuf, S.b_xn], writes=[b_ps[pi]], accum=(k > 0))
        sink(jj, ps[pi], b_ps[pi])


def emit_proj_tm(ctx, S, wbuf, b_wbuf, w_bf, c0, ncols, ps, b_ps, TT, sink):
    nc, P = ctx.nc, ctx.P
    P.op("sp", lambda: nc.sync.dma_start(
        out=wbuf[:, :, 0:ncols], in_=w_bf.rearrange("(k p) f -> p k f", p=128)[:, :, c0:c0 + ncols]),
        reads=[ctx.b_wcv], writes=[b_wbuf], stream="wst_" + b_wbuf.name)
    for i in range(TT // 128):
        pi = i % len(ps)
        for k in range(KC):
            P.op("pe", lambda k=k, i=i, pi=pi: nc.tensor.matmul(
                ps[pi][:, 0:ncols], S.xn[:, k, i * 128:(i + 1) * 128], wbuf[:, k, 0:ncols],
                start=(k == 0), stop=(k == KC - 1)),
                reads=[b_wbuf, S.b_xn], writes=[b_ps[pi]], accum=(k > 0))
        sink(i, ps[pi], b_ps[pi])


class MemSb:
    def __init__(self, nc, NH):
        self.NH = NH
        self.KT = nc.alloc_sbuf_tensor("m_KT", [128, 2 * NH, 256], F32)
        self.KnT = nc.alloc_sbuf_tensor("m_KnT", [128, 2 * NH, 256], BF16)
        self.V = nc.alloc_sbuf_tensor("m_V", [128, 2, 256 * NH], BF16)
        self.kg = nc.alloc_sbuf_tensor("m_kg", [128, 2], F32)
        self.qg = nc.alloc_sbuf_tensor("m_qg", [128, 2], F32)
        self.qT = nc.alloc_sbuf_tensor("m_qT", [128, 2 * NH, 512], F32)
        self.qn = nc.alloc_sbuf_tensor("m_qn", [128, 2 * NH, 512], BF16)
        self.sq = [nc.alloc_sbuf_tensor("m_sq%d" % i, [128, 512], BF16) for i in range(2)]
        self.rs = nc.alloc_sbuf_tensor("m_rs", [128, 512], F32)
        self.pT = [nc.alloc_sbuf_tensor("m_pT%d" % i, [128, 512], BF16) for i in range(2)]
        self.rden = nc.alloc_sbuf_tensor("m_rden", [128, 512], F32)
        self.o = [nc.alloc_sbuf_tensor("m_o%d" % i, [128, 512], BF16) for i in range(2)]
        B = Buf
        self.b_KT, self.b_KnT, self.b_V, self.b_g = B("KT"), B("KnT"), B("V"), B("mg")
        self.b_qT, self.b_qn, self.b_rs, self.b_rden = B("qT"), B("qn"), B("rs"), B("rden")
        self.b_sq = [B("msq0"), B("msq1")]
        self.b_pT = [B("pT0"), B("pT1")]
        self.b_o = [B("mo0"), B("mo1")]


def emit_mem_prep(ctx, M, N, memT_ap, gmem_ap, wk_bf, wv_bf, kg_ap, qg_ap, wbuf, b_wbuf, ps, b_ps):
    nc, P = ctx.nc, ctx.P
    NH = M.NH
    b_in = Buf("memin")
    P.op("sp", lambda: nc.sync.dma_start(out=M.kg[:], in_=kg_ap.rearrange("(c p) -> p c", p=128),
                                         allow_slow_non_contiguous=True), writes=[M.b_g], stream="mg")
    P.op("sp", lambda: nc.sync.dma_start(out=M.qg[:], in_=qg_ap.rearrange("(c p) -> p c", p=128),
                                         allow_slow_non_contiguous=True), writes=[M.b_g], stream="mg", accum=True)
    emit_norm_tile(ctx, N, memT_ap.rearrange("(k p) t -> p k t", p=128), gmem_ap, 256, b_in)

    def ksink(jj, psap, bps):
        P.op("act", lambda: nc.scalar.copy(out=M.KT[:, jj, :], in_=psap[:, 0:256]),
             reads=[bps], writes=[M.b_KT], accum=(jj > 0))
    emit_proj_fm(ctx, N, wbuf, b_wbuf, wk_bf, 0, 256 * NH, ps, b_ps, 256, ksink)
    for hh in range(NH):
        for c in range(2):
            P.op("act", lambda c=c, hh=hh: nc.scalar.activation(out=M.sq[c][:, 0:256], in_=M.KT[:, 2 * hh + c, :],
                                                                func=AF.Square),
                 reads=[M.b_KT], writes=[M.b_sq[c]])
            P.op("pe", lambda c=c: nc.tensor.matmul(ps[0][:, 0:256], ctx.ones_bf[:], M.sq[c][:, 0:256],
                                                  start=(c == 0), stop=(c == 1)),
                 reads=[M.b_sq[c], ctx.b_ones], writes=[b_ps[0]], accum=(c > 0))
        P.op("act", lambda: nc.scalar.activation(out=M.rs[:, 0:256], in_=ps[0][:, 0:256], func=AF.Sqrt,
                                                 bias=ctx.eps_t[:], scale=1.0 / 256),
             reads=[b_ps[0], ctx.b_eps], writes=[M.b_rs])
        P.op("dve", lambda: nc.vector.reciprocal(out=M.rs[:, 0:256], in_=M.rs[:, 0:256]),
             reads=[M.b_rs], writes=[M.b_rs])
        for c in range(2):
            P.op("dve", lambda c=c, hh=hh: nc.vector.scalar_tensor_tensor(
                out=M.KnT[:, 2 * hh + c, :], in0=M.KT[:, 2 * hh + c, :], scalar=M.kg[:, c:c + 1],
                in1=M.rs[:, 0:256], op0=ALU.mult, op1=ALU.mult),
                reads=[M.b_KT, M.b_g, M.b_rs], writes=[M.b_KnT], accum=True)

    def vsink(i, psap, bps):
        P.op("act", lambda: nc.scalar.copy(out=M.V[:, i, :], in_=psap[:, 0:256 * NH]),
             reads=[bps], writes=[M.b_V], accum=True)
    emit_proj_tm(ctx, N, wbuf, b_wbuf, wv_bf, 0, 256 * NH, ps, b_ps, 256, vsink)


def emit_mem_attn_tile(ctx, M, ps, b_ps, out_ap_fn, b_out, stream):
    nc, P = ctx.nc, ctx.P
    NH = M.NH
    for hh in range(NH):
        for c in range(2):
            P.op("act", lambda c=c, hh=hh: nc.scalar.activation(out=M.sq[c][:], in_=M.qT[:, 2 * hh + c, :],
                                                                func=AF.Square),
                 reads=[M.b_qT], writes=[M.b_sq[c]])
            P.op("pe", lambda c=c: nc.tensor.matmul(ps[0][:], ctx.ones_bf[:], M.sq[c][:],
                                                  start=(c == 0), stop=(c == 1)),
                 reads=[M.b_sq[c], ctx.b_ones], writes=[b_ps[0]], accum=(c > 0))
        P.op("act", lambda: nc.scalar.activation(out=M.rs[:], in_=ps[0][:], func=AF.Sqrt,
                                                 bias=ctx.eps_t[:], scale=1.0 / 256),
             reads=[b_ps[0], ctx.b_eps], writes=[M.b_rs])
        P.op("dve", lambda: nc.vector.reciprocal(out=M.rs[:], in_=M.rs[:]), reads=[M.b_rs], writes=[M.b_rs])
        for c in range(2):
            P.op("dve", lambda c=c, hh=hh: nc.vector.scalar_tensor_tensor(
                out=M.qn[:, 2 * hh + c, :], in0=M.qT[:, 2 * hh + c, :], scalar=M.qg[:, c:c + 1],
                in1=M.rs[:], op0=ALU.mult, op1=ALU.mult),
                reads=[M.b_qT, M.b_g, M.b_rs], writes=[M.b_qn], accum=True)
        for mi in range(2):
            for c in range(2):
                P.op("pe", lambda c=c, mi=mi, hh=hh: nc.tensor.matmul(
                    ps[1][:], M.KnT[:, 2 * hh + c, mi * 128:(mi + 1) * 128], M.qn[:, 2 * hh + c, :],
                    start=(c == 0), stop=(c == 1)),
                    reads=[M.b_KnT, M.b_qn], writes=[b_ps[1]], accum=(c > 0))
            P.op("act", lambda mi=mi: nc.scalar.activation(out=M.pT[mi][:], in_=ps[1][:], func=AF.Exp, scale=1.0 / 16),
                 reads=[b_ps[1]], writes=[M.b_pT[mi]])
        for mi in range(2):
            P.op("pe", lambda mi=mi: nc.tensor.matmul(ps[0][:], ctx.ones_bf[:], M.pT[mi][:],
                                                    start=(mi == 0), stop=(mi == 1)),
                 reads=[M.b_pT[mi], ctx.b_ones], writes=[b_ps[0]], accum=(mi > 0))
        P.op("dve", lambda: nc.vector.reciprocal(out=M.rden[:], in_=ps[0][:]), reads=[b_ps[0]], writes=[M.b_rden])
        for dc in range(2):
            for mi in range(2):
                P.op("pe", lambda mi=mi, dc=dc, hh=hh: nc.tensor.matmul(
                    ps[1][:], M.V[:, mi, hh * 256 + dc * 128: hh * 256 + (dc + 1) * 128], M.pT[mi][:],
                    start=(mi == 0), stop=(mi == 1)),
                    reads=[M.b_V, M.b_pT[mi]], writes=[b_ps[1]], accum=(mi > 0))
            P.op("dve", lambda dc=dc: nc.vector.tensor_tensor(out=M.o[dc][:], in0=ps[1][:], in1=M.rden[:], op=ALU.mult),
                 reads=[b_ps[1], M.b_rden], writes=[M.b_o[dc]])
            P.op("sp", lambda dc=dc, hh=hh: nc.sync.dma_start(out=out_ap_fn(hh, dc), in_=M.o[dc][:]),
                 reads=[M.b_o[dc]], writes=[b_out], stream=stream + str(dc), accum=True)


SSD_NC = 4656


def bc3(ap2, n):
    return ap2.unsqueeze(2).to_broadcast([ap2.shape[0], ap2.shape[1], n])


def build_ssd(S, debug=False):
    nc = bass.Bass("TRN2", target_bir_lowering=False)
    dbg_streams = []

    def dbg(name, ap, buf, shape, dtype=F32):
        if not debug:
            return
        o = nc.dram_tensor("dbg_" + name, shape, dtype, kind="ExternalOutput").ap()
        P.op("sp", lambda: nc.sync.dma_start(out=o, in_=ap), reads=[buf], writes=[Buf("d")], stream="dbg_" + name)
        dbg_streams.append("dbg_" + name)
    dt_ = nc.dram_tensor
    xT = dt_("xT", [D, S], F32, kind="ExternalInput").ap()
    gmix = dt_("gmix", [D], F32, kind="ExternalInput").ap()
    wc = dt_("wc", [D, SSD_NC], F32, kind="ExternalInput").ap()
    conv_w = dt_("conv_w", [2560, 5], F32, kind="ExternalInput").ap()
    conv_b = dt_("conv_b", [2560], F32, kind="ExternalInput").ap()
    dt_bias = dt_("dt_bias", [1, 48], F32, kind="ExternalInput").ap()
    a_log = dt_("a_log", [1, 48], F32, kind="ExternalInput").ap()
    dskip = dt_("dskip", [1, 24], F32, kind="ExternalInput").ap()
    norm_g = dt_("norm_g", [1, 1536], F32, kind="ExternalInput").ap()
    memT = dt_("memT", [D, 256], F32, kind="ExternalInput").ap()
    gmem = dt_("gmem", [D], F32, kind="ExternalInput").ap()
    wk = dt_("wk", [D, 512], F32, kind="ExternalInput").ap()
    wv = dt_("wv", [D, 512], F32, kind="ExternalInput").ap()
    kg = dt_("kg", [256], F32, kind="ExternalInput").ap()
    qg = dt_("qg", [256], F32, kind="ExternalInput").ap()
    cst = dt_("cst", [128, 7, 128], F32, kind="ExternalInput").ap()
    yT = dt_("yT", [D, S], BF16, kind="ExternalOutput").ap()
    wc_bf = dt_("wc_bf", [D, SSD_NC], BF16, kind="Internal").ap()
    wk_bf = dt_("wk_bf", [D, 512], BF16, kind="Internal").ap()
    wv_bf = dt_("wv_bf", [D, 512], BF16, kind="Internal").ap()
    xbc_pre = dt_("xbc_pre", [2560, S], F32, kind="Internal").ap()
    zs = dt_("zs", [S, 1536], BF16, kind="Internal").ap()
    dtr = dt_("dtr", [S, 48], F32, kind="Internal").ap()
    y1 = dt_("y1", [S, 1536], F32, kind="Internal").ap()

    ctx = Ctx(); ctx.nc = nc; P = ctx.P = Prog(nc)
    mk_consts(ctx)
    load_consts(ctx, cst)
    ones_f = nc.alloc_sbuf_tensor("ones_f", [128, 128], F32)
    P.op("pool", lambda: nc.gpsimd.memset(ones_f[:], 1.0), writes=[ctx.b_ones], accum=True)
    convert_weight(ctx, wc, wc_bf, 256)
    convert_weight(ctx, wk, wk_bf, 512)
    convert_weight(ctx, wv, wv_bf, 512)
    ctx.b_wcv = Buf("wcv")
    P.barrier()

    PS = [nc.alloc_psum_tensor("ps%d" % i, [128, 512], F32) for i in range(7)]
    PSB = nc.alloc_psum_tensor("psb", [128, 1024], BF16)
    bPS = [Buf("ps%d" % i) for i in range(7)]
    bPSB = Buf("psb")
    N = NormSb(nc, 512, PS[0])
    N.b_ps_ss = bPS[0]
    M = MemSb(nc, 2)
    wbuf = [nc.alloc_sbuf_tensor("wbuf%d" % i, [128, KC, 512], BF16) for i in range(2)]
    b_wbuf = [Buf("wb0"), Buf("wb1")]
    stg = nc.alloc_sbuf_tensor("stg", [128, 4, 512], F32)
    b_stg = Buf("stg")
    zst = nc.alloc_sbuf_tensor("zst", [128, 4, 512], BF16)
    b_zst = Buf("zst")
    dst = nc.alloc_sbuf_tensor("dst", [128, 4, 48], F32)
    b_dst = Buf("dst")
    b_xin = Buf("xin")
    b_pre_d, b_zs_d, b_dtr_d, b_y1_d, b_out = Buf("pre_d"), Buf("zs_d"), Buf("dtr_d"), Buf("y1_d"), Buf("out")

    emit_mem_prep(ctx, M, N, memT, gmem, wk_bf, wv_bf, kg, qg, wbuf[0], b_wbuf[0], [PS[1], PS[2]], [bPS[1], bPS[2]])

    TT = 512
    yT_v = yT.rearrange("(j p) t -> p j t", p=128)
    pre_v = xbc_pre.rearrange("(j p) t -> p j t", p=128)
    zs_v = zs.rearrange("(i p) f -> p i f", p=128)
    dtr_v = dtr.rearrange("(i p) f -> p i f", p=128)
    y1_v = y1.rearrange("(i p) f -> p i f", p=128)
    wi = [0]

    def nextw():
        wi[0] += 1
        return wbuf[wi[0] % 2], b_wbuf[wi[0] % 2]

    for t in range(S // TT):
        t0 = t * TT
        emit_norm_tile(ctx, N, xT.rearrange("(k p) t -> p k t", p=128)[:, :, t0:t0 + TT], gmix, TT, b_xin)
        for gi in range(5):
            wb, bwb = nextw()

            def sink(jj, psap, bps):
                P.op("act", lambda: nc.scalar.copy(out=stg[:, jj, :], in_=psap[:]),
                     reads=[bps], writes=[b_stg], accum=(jj > 0))
            emit_proj_fm(ctx, N, wb, bwb, wc_bf, 1536 + gi * 512, 512, [PS[1], PS[2]], [bPS[1], bPS[2]], TT, sink)
            P.op("sp", lambda gi=gi, t0=t0: nc.sync.dma_start(out=pre_v[:, gi * 4:(gi + 1) * 4, t0:t0 + TT], in_=stg[:]),
                 reads=[b_stg], writes=[b_pre_d], stream="st_pre", accum=True)
        wb, bwb = nextw()

        def qsink(jj, psap, bps):
            P.op("act", lambda: nc.scalar.copy(out=M.qT[:, jj, :], in_=psap[:]),
                 reads=[bps], writes=[M.b_qT], accum=(jj > 0))
        emit_proj_fm(ctx, N, wb, bwb, wc_bf, 4144, 512, [PS[1], PS[2]], [bPS[1], bPS[2]], TT, qsink)
        emit_mem_attn_tile(ctx, M, [PS[3], PS[4]], [bPS[3], bPS[4]],
                           lambda hh, dc, t0=t0: yT_v[:, 12 + hh * 2 + dc, t0:t0 + TT], b_out, "st_mo")
        for zi in range(3):
            wb, bwb = nextw()

            def zsink(i, psap, bps):
                P.op("act", lambda: nc.scalar.activation(out=zst[:, i, :], in_=psap[:], func=AF.Silu),
                     reads=[bps], writes=[b_zst], accum=(i > 0))
            emit_proj_tm(ctx, N, wb, bwb, wc_bf, zi * 512, 512, [PS[1], PS[2]], [bPS[1], bPS[2]], TT, zsink)
            P.op("sp", lambda zi=zi, t0=t0: nc.sync.dma_start(
                out=zs_v[:, t0 // 128:t0 // 128 + 4, zi * 512:(zi + 1) * 512], in_=zst[:]),
                reads=[b_zst], writes=[b_zs_d], stream="st_zs", accum=True)
        wb, bwb = nextw()

        def dsink(i, psap, bps):
            P.op("act", lambda: nc.scalar.copy(out=dst[:, i, :], in_=psap[:, 0:48]),
                 reads=[bps], writes=[b_dst], accum=(i > 0))
        emit_proj_tm(ctx, N, wb, bwb, wc_bf, 4096, 48, [PS[1], PS[2]], [bPS[1], bPS[2]], TT, dsink)
        P.op("sp", lambda t0=t0: nc.sync.dma_start(out=dtr_v[:, t0 // 128:t0 // 128 + 4, :], in_=dst[:]),
             reads=[b_dst], writes=[b_dtr_d], stream="st_dt", accum=True)

    P.barrier()

    sb = nc.alloc_sbuf_tensor
    pre = sb("pre", [128, 20, 132], F32)
    acc = sb("acc", [128, 20, 128], F32)
    fm = sb("fm", [128, 20, 128], BF16)
    xs_tok = sb("xs_tok", [128, 1536], BF16)
    B_tok = sb("B_tok", [128, 512], BF16)
    cw = sb("cw", [128, 20, 5], F32)
    cb = sb("cb", [128, 20], F32)
    dtb = sb("dtb", [128, 48], F32)
    A_bc = sb("A_bc", [128, 48], F32)
    D_bc = sb("D_bc", [128, 24], F32)
    ng_bc = sb("ng_bc", [128, 1536], F32)
    dtraw = sb("dtraw", [128, 48], F32)
    dtx = sb("dtx", [128, 24], F32)
    dtv = sb("dtv", [128, 24], F32)
    av = sb("av", [128, 24], F32)
    cs_sb = sb("cs_sb", [128, 24], F32)
    dout = sb("dout", [128, 24], F32)
    din = sb("din", [128, 24], F32)
    cdec = sb("cdec", [128, 24], F32)
    X = sb("X", [128, 24, 64], BF16)
    Xd = sb("Xd", [128, 24, 64], BF16)
    cbs = sb("cbs", [128, 4, 128], F32)
    la = [sb("la%d" % i, [128, 3, 128], F32) for i in range(2)]
    E = [sb("E%d" % i, [128, 3, 128], F32) for i in range(2)]
    MT = [sb("MT%d" % i, [128, 3, 128], BF16) for i in range(2)]
    t1 = sb("t1", [128, 6, 64], F32)
    xv = lambda i: N.x[:, 3 * i:3 * i + 3, :].rearrange("p k t -> p (k t)")
    ypart = xv(0)
    y1t = xv(1)
    zt = sb("zt", [128, 1536], BF16)
    t2 = xv(2)
    ss4 = sb("ss4", [128, 4], F32)
    yn = sb("yn", [128, 1536], BF16)
    yTt = sb("yTt", [128, 12, 128], BF16)
    H = [xv(3), xv(4)]
    Hbf = [sb("Hbf%d" % i, [128, 1536], BF16) for i in range(2)]
    B_ = Buf
    b = {n: B_(n) for n in ["pre", "acc", "fm", "xs_tok", "B_tok", "par", "dtraw", "dtx", "dtv", "av", "cs_sb", "dout",
                            "din", "cdec", "X", "Xd", "cbs", "la0", "la1", "E0", "E1", "MT0", "MT1", "t1", "ypart",
                            "y1t", "zt", "t2", "ss4", "yn", "yTt", "H0", "H1", "Hbf0", "Hbf1"]}
    P.op("sp", lambda: nc.sync.dma_start(out=cw[:], in_=conv_w.rearrange("(j p) k -> p j k", p=128),
                                         allow_slow_non_contiguous=True), writes=[b["par"]], stream="par")
    P.op("sp", lambda: nc.sync.dma_start(out=cb[:], in_=conv_b.rearrange("(j p) -> p j", p=128),
                                         allow_slow_non_contiguous=True), writes=[b["par"]], stream="par", accum=True)
    for (dst_t, src, n) in [(dtb, dt_bias, 48), (A_bc, a_log, 48), (D_bc, dskip, 24), (ng_bc, norm_g, 1536)]:
        P.op("sp", lambda dst_t=dst_t, src=src, n=n: nc.sync.dma_start(out=dst_t[:], in_=src.to_broadcast([128, n])),
             writes=[b["par"]], stream="par", accum=True)
    P.op("act", lambda: nc.scalar.activation(out=A_bc[:], in_=A_bc[:], func=AF.Exp), reads=[b["par"]], writes=[b["par"]])
    P.op("dve", lambda: nc.vector.tensor_scalar(out=A_bc[:], in0=A_bc[:], scalar1=-1.0, scalar2=None, op0=ALU.mult),
         reads=[b["par"]], writes=[b["par"]])

    NCH = S // 128
    ps_small, b_small = PS[0], bPS[0]
    ps_cb, b_cb = PS[1], bPS[1]
    ps_dd, b_dd = [PS[2], PS[3]], [bPS[2], bPS[3]]
    ps_yd, b_yd = PS[4], bPS[4]
    ps_yo, b_yo = PS[5], bPS[5]
    ps_st, b_st = PS[6], bPS[6]

    for d in range(2):
        P.op("pool", lambda d=d: nc.gpsimd.memset(H[d][:], 0.0), writes=[b["H%d" % d]])
        P.op("pool", lambda d=d: nc.gpsimd.memset(Hbf[d][:], 0.0), writes=[b["Hbf%d" % d]])
        order = range(NCH) if d == 0 else range(NCH - 1, -1, -1)
        for c in order:
            lo, hi = max(0, c * 128 - 2), min(S, c * 128 + 130)
            off = lo - (c * 128 - 2)
            first = True
            if c == 0:
                P.op("pool", lambda: nc.gpsimd.memset(pre[:, :, 0:2], 0.0), writes=[b["pre"]])
                first = False
            if c == NCH - 1:
                P.op("pool", lambda: nc.gpsimd.memset(pre[:, :, 130:132], 0.0), writes=[b["pre"]], accum=not first)
                first = False
            P.op("sp", lambda lo=lo, hi=hi, off=off: nc.sync.dma_start(out=pre[:, :, off:off + hi - lo], in_=pre_v[:, :, lo:hi]),
                 reads=[b_pre_d], writes=[b["pre"]], stream="ld_pre", accum=not first)
            P.op("sp", lambda c=c: nc.sync.dma_start(out=dtraw[:], in_=dtr[c * 128:(c + 1) * 128, :]),
                 reads=[b_dtr_d], writes=[b["dtraw"]], stream="ld_dt")
            for j in range(20):
                P.op("dve", lambda j=j: nc.vector.tensor_scalar(
                    out=acc[:, j, :], in0=pre[:, j, 0:128], scalar1=cw[:, j, 0:1], scalar2=cb[:, j:j + 1],
                    op0=ALU.mult, op1=ALU.add), reads=[b["pre"], b["par"]], writes=[b["acc"]], accum=(j > 0))
                for k in range(1, 5):
                    P.op("dve", lambda j=j, k=k: nc.vector.scalar_tensor_tensor(
                        out=acc[:, j, :], in0=pre[:, j, k:k + 128], scalar=cw[:, j, k:k + 1], in1=acc[:, j, :],
                        op0=ALU.mult, op1=ALU.add), reads=[b["pre"], b["par"]], writes=[b["acc"]], accum=True)
            P.op("act", lambda: nc.scalar.activation(out=fm[:], in_=acc[:], func=AF.Silu),
                 reads=[b["acc"]], writes=[b["fm"]])
            for jb in range(4):
                for q in range(4):
                    P.op("pe", lambda jb=jb, q=q: nc.tensor.transpose(
                        out=PSB[:, q * 128:(q + 1) * 128], in_=fm[:, jb * 4 + q, :], identity=ctx.ident_bf[:]),
                        reads=[b["fm"], ctx.b_cf], writes=[bPSB], accum=(q > 0))
                dstt = xs_tok[:, jb * 512:(jb + 1) * 512] if jb < 3 else B_tok[:]
                P.op("act", lambda dstt=dstt: nc.scalar.copy(out=dstt, in_=PSB[:, 0:512]),
                     reads=[bPSB], writes=[b["xs_tok"] if jb < 3 else b["B_tok"]], accum=(0 < jb < 3))
            P.op("dve", lambda d=d: nc.vector.tensor_tensor(out=dtx[:], in0=dtraw[:, d * 24:(d + 1) * 24],
                                                            in1=dtb[:, d * 24:(d + 1) * 24], op=ALU.add),
                 reads=[b["dtraw"], b["par"]], writes=[b["dtx"]])
            P.op("act", lambda: nc.scalar.activation(out=dtx[:], in_=dtx[:], func=AF.Exp), reads=[b["dtx"]], writes=[b["dtx"]])
            P.op("act", lambda: nc.scalar.activation(out=dtv[:], in_=dtx[:], func=AF.Ln, bias=1.0),
                 reads=[b["dtx"]], writes=[b["dtv"]])
            P.op("dve", lambda d=d: nc.vector.tensor_tensor(out=av[:], in0=dtv[:], in1=A_bc[:, d * 24:(d + 1) * 24], op=ALU.mult),
                 reads=[b["dtv"], b["par"]], writes=[b["av"]])
            P.op("pe", lambda d=d: nc.tensor.matmul(ps_small[:, 0:24], ctx.cf[:, 1 + d, :], av[:], start=True, stop=True),
                 reads=[b["av"], ctx.b_cf], writes=[b_small])
            P.op("pe", lambda: nc.tensor.matmul(ps_small[:, 32:56], ones_f[:], av[:], start=True, stop=True),
                 reads=[b["av"], ctx.b_ones], writes=[b_small], accum=True)
            P.op("act", lambda: nc.scalar.copy(out=cs_sb[:], in_=ps_small[:, 0:24]), reads=[b_small], writes=[b["cs_sb"]])
            P.op("act", lambda: nc.scalar.activation(out=dout[:], in_=ps_small[:, 0:24], func=AF.Exp),
                 reads=[b_small], writes=[b["dout"]])
            P.op("act", lambda: nc.scalar.activation(out=cdec[:], in_=ps_small[:, 32:56], func=AF.Exp),
                 reads=[b_small], writes=[b["cdec"]])
            P.op("dve", lambda: nc.vector.tensor_tensor(out=din[:], in0=ps_small[:, 32:56], in1=cs_sb[:], op=ALU.subtract),
                 reads=[b_small, b["cs_sb"]], writes=[b["din"]])
            P.op("act", lambda: nc.scalar.activation(out=din[:], in_=din[:], func=AF.Exp), reads=[b["din"]], writes=[b["din"]])
            P.op("dve", lambda: nc.vector.tensor_tensor(out=X[:], in0=xs_tok[:].rearrange("p (h e) -> p h e", e=64),
                                                        in1=bc3(dtv[:], 64), op=ALU.mult),
                 reads=[b["xs_tok"], b["dtv"]], writes=[b["X"]])
            P.op("dve", lambda: nc.vector.tensor_tensor(out=Xd[:], in0=X[:], in1=bc3(din[:], 64), op=ALU.mult),
                 reads=[b["X"], b["din"]], writes=[b["Xd"]])
            for g in range(4):
                P.op("pe", lambda g=g: nc.tensor.matmul(ps_cb[:, g * 128:(g + 1) * 128], fm[:, 12 + g, :], fm[:, 16 + g, :],
                                                        start=True, stop=True),
                     reads=[b["fm"]], writes=[b_cb], accum=(g > 0))
            P.op("act", lambda: nc.scalar.copy(out=cbs[:], in_=ps_cb[:].rearrange("p (g l) -> p g l", g=4)),
                 reads=[b_cb], writes=[b["cbs"]])
            for g in range(4):
                for hb in range(2):
                    for r3 in range(3):
                        h = g * 6 + hb * 3 + r3
                        P.op("dve", lambda hb=hb, r3=r3, h=h, d=d: nc.vector.tensor_scalar(
                            out=la[hb][:, r3, :], in0=ctx.cf[:, 3 + d, :], scalar1=av[:, h:h + 1], scalar2=None, op0=ALU.mult),
                            reads=[b["av"], ctx.b_cf], writes=[b["la%d" % hb]], accum=(r3 > 0))
                    for r3 in range(3):
                        P.op("pe", lambda hb=hb, r3=r3, d=d: nc.tensor.matmul(
                            ps_dd[hb][:, r3 * 128:(r3 + 1) * 128], la[hb][:, r3, :], ctx.cf[:, 1 + d, :], start=True, stop=False),
                            reads=[b["la%d" % hb], ctx.b_cf], writes=[b_dd[hb]], accum=(r3 > 0))
                        P.op("pe", lambda hb=hb, r3=r3, d=d: nc.tensor.matmul(
                            ps_dd[hb][:, r3 * 128:(r3 + 1) * 128], ctx.cf[:, 0, :], ctx.cf[:, 5 + d, :], start=False, stop=True),
                            reads=[ctx.b_cf], writes=[b_dd[hb]], accum=True)
                    P.op("act", lambda hb=hb: nc.scalar.activation(
                        out=E[hb][:], in_=ps_dd[hb][:, 0:384].rearrange("p (r l) -> p r l", r=3), func=AF.Exp),
                        reads=[b_dd[hb]], writes=[b["E%d" % hb]])
                    P.op("dve", lambda hb=hb, g=g: nc.vector.tensor_tensor(
                        out=MT[hb][:], in0=E[hb][:], in1=cbs[:, g, :].unsqueeze(1).to_broadcast([128, 3, 128]), op=ALU.mult),
                        reads=[b["E%d" % hb], b["cbs"]], writes=[b["MT%d" % hb]])
                    for r3 in range(3):
                        h = g * 6 + hb * 3 + r3
                        P.op("pe", lambda hb=hb, r3=r3, h=h: nc.tensor.matmul(
                            ps_yd[:, (hb * 3 + r3) * 64:(hb * 3 + r3 + 1) * 64], MT[hb][:, r3, :], X[:, h, :], start=True, stop=True),
                            reads=[b["MT%d" % hb], b["X"]], writes=[b_yd], accum=not (hb == 0 and r3 == 0))
                P.op("pe", lambda g=g, d=d: nc.tensor.matmul(ps_yo[:, 0:384], fm[:, 16 + g, :], Hbf[d][:, g * 384:(g + 1) * 384],
                                                             start=True, stop=True),
                     reads=[b["fm"], b["Hbf%d" % d]], writes=[b_yo])
                P.op("dve", lambda g=g: nc.vector.tensor_tensor(
                    out=t1[:], in0=ps_yo[:, 0:384].rearrange("p (h e) -> p h e", e=64), in1=bc3(dout[:, g * 6:(g + 1) * 6], 64),
                    op=ALU.mult), reads=[b_yo, b["dout"]], writes=[b["t1"]])
                P.op("dve", lambda g=g: nc.vector.tensor_tensor(
                    out=ypart[:, g * 384:(g + 1) * 384], in0=t1[:].rearrange("p h e -> p (h e)"), in1=ps_yd[:, 0:384], op=ALU.add),
                    reads=[b["t1"], b_yd], writes=[b["ypart"]], accum=(g > 0))
                P.op("pe", lambda g=g: nc.tensor.matmul(ps_st[:, 0:384], B_tok[:, g * 128:(g + 1) * 128],
                                                        Xd[:, g * 6:(g + 1) * 6, :].rearrange("p h e -> p (h e)"), start=True, stop=True),
                     reads=[b["B_tok"], b["Xd"]], writes=[b_st])
                Hg = H[d][:, g * 384:(g + 1) * 384]
                P.op("dve", lambda Hg=Hg, g=g: nc.vector.tensor_tensor(
                    out=Hg.rearrange("p (h e) -> p h e", e=64), in0=Hg.rearrange("p (h e) -> p h e", e=64),
                    in1=bc3(cdec[:, g * 6:(g + 1) * 6], 64), op=ALU.mult),
                    reads=[b["H%d" % d], b["cdec"]], writes=[b["H%d" % d]])
                P.op("dve", lambda Hg=Hg: nc.vector.tensor_tensor(out=Hg, in0=Hg, in1=ps_st[:, 0:384], op=ALU.add),
                     reads=[b["H%d" % d], b_st], writes=[b["H%d" % d]])
                P.op("act", lambda Hg=Hg, g=g, d=d: nc.scalar.copy(out=Hbf[d][:, g * 384:(g + 1) * 384], in_=Hg),
                     reads=[b["H%d" % d]], writes=[b["Hbf%d" % d]])
            if d == 0 and c == 0:
                dbg("dtraw", dtraw[:], b["dtraw"], [128, 48])
                dbg("dtv", dtv[:], b["dtv"], [128, 24])
                dbg("av", av[:], b["av"], [128, 24])
                dbg("cs", cs_sb[:], b["cs_sb"], [128, 24])
                dbg("dout", dout[:], b["dout"], [128, 24])
                dbg("din", din[:], b["din"], [128, 24])
                dbg("cdec", cdec[:], b["cdec"], [128, 24])
                dbg("cbs", cbs[:], b["cbs"], [128, 4, 128])
                dbg("E1", E[1][:], b["E1"], [128, 3, 128])
                dbg("xs", xs_tok[:], b["xs_tok"], [128, 1536], BF16)
                dbg("fm", fm[:], b["fm"], [128, 20, 128], BF16)
                dbg("acc", acc[:], b["acc"], [128, 20, 128])
                dbg("ypart", ypart[:], b["ypart"], [128, 1536])
                dbg("H0", H[0][:], b["H0"], [128, 1536])
            if d == 0:
                P.op("sp", lambda c=c: nc.sync.dma_start(out=y1[c * 128:(c + 1) * 128, :], in_=ypart[:]),
                     reads=[b["ypart"]], writes=[b_y1_d], stream="st_y1", accum=True)
            else:
                P.op("sp", lambda c=c: nc.sync.dma_start(out=y1t[:], in_=y1[c * 128:(c + 1) * 128, :]),
                     reads=[b_y1_d], writes=[b["y1t"]], stream="ld_y1")
                P.op("sp", lambda c=c: nc.sync.dma_start(out=zt[:], in_=zs[c * 128:(c + 1) * 128, :]),
                     reads=[b_zs_d], writes=[b["zt"]], stream="ld_zs")
                P.op("dve", lambda: nc.vector.tensor_tensor(out=ypart[:], in0=ypart[:], in1=y1t[:], op=ALU.add),
                     reads=[b["ypart"], b["y1t"]], writes=[b["ypart"]])
                P.op("dve", lambda: nc.vector.tensor_tensor(
                    out=t2[:].rearrange("p (h e) -> p h e", e=64), in0=xs_tok[:].rearrange("p (h e) -> p h e", e=64),
                    in1=bc3(D_bc[:], 64), op=ALU.mult), reads=[b["xs_tok"], b["par"]], writes=[b["t2"]])
                P.op("dve", lambda: nc.vector.tensor_tensor(out=ypart[:], in0=ypart[:], in1=t2[:], op=ALU.add),
                     reads=[b["ypart"], b["t2"]], writes=[b["ypart"]])
                P.op("dve", lambda: nc.vector.tensor_tensor(out=ypart[:], in0=ypart[:], in1=zt[:], op=ALU.mult),
                     reads=[b["ypart"], b["zt"]], writes=[b["ypart"]])
                P.op("act", lambda: nc.scalar.activation(out=t2[:], in_=ypart[:], func=AF.Square),
                     reads=[b["ypart"]], writes=[b["t2"]])
                P.op("dve", lambda: nc.vector.tensor_reduce(out=ss4[:], in_=t2[:].rearrange("p (g f) -> p g f", g=4),
                                                            axis=AX.X, op=ALU.add), reads=[b["t2"]], writes=[b["ss4"]])
                P.op("act", lambda: nc.scalar.activation(out=ss4[:], in_=ss4[:], func=AF.Sqrt, bias=ctx.eps_t[:], scale=1.0 / 384),
                     reads=[b["ss4"], ctx.b_eps], writes=[b["ss4"]])
                P.op("dve", lambda: nc.vector.reciprocal(out=ss4[:], in_=ss4[:]), reads=[b["ss4"]], writes=[b["ss4"]])
                P.op("dve", lambda: nc.vector.tensor_tensor(
                    out=t2[:].rearrange("p (g f) -> p g f", g=4), in0=ypart[:].rearrange("p (g f) -> p g f", g=4),
                    in1=bc3(ss4[:], 384), op=ALU.mult), reads=[b["ypart"], b["ss4"]], writes=[b["t2"]])
                P.op("dve", lambda: nc.vector.tensor_tensor(out=yn[:], in0=t2[:], in1=ng_bc[:], op=ALU.mult),
                     reads=[b["t2"], b["par"]], writes=[b["yn"]])
                for jb in range(3):
                    for q in range(4):
                        P.op("pe", lambda jb=jb, q=q: nc.tensor.transpose(
                            out=PSB[:, q * 128:(q + 1) * 128], in_=yn[:, (jb * 4 + q) * 128:(jb * 4 + q + 1) * 128],
                            identity=ctx.ident_bf[:]), reads=[b["yn"], ctx.b_cf], writes=[bPSB], accum=(q > 0))
                    P.op("act", lambda jb=jb: nc.scalar.copy(out=yTt[:, jb * 4:(jb + 1) * 4, :],
                                                             in_=PSB[:, 0:512].rearrange("p (q t) -> p q t", q=4)),
                         reads=[bPSB], writes=[b["yTt"]], accum=(jb > 0))
                P.op("sp", lambda c=c: nc.sync.dma_start(out=yT_v[:, 0:12, c * 128:(c + 1) * 128], in_=yTt[:]),
                     reads=[b["yTt"]], writes=[b_out], stream="st_y", accum=True)
    st = P.emit(final_wait_streams=["st_y", "st_mo0", "st_mo1"] + dbg_streams)
    return nc, st


def emit_transpose_in(ctx, S, x_tok, xT, T, PSb, bPSb, b_out):
    nc, P = ctx.nc, ctx.P
    for t in range(T // 512):
        for i in range(4):
            r0 = t * 512 + i * 128
            P.op("sp", lambda r0=r0: nc.sync.dma_start(out=ctx.tok[:], in_=x_tok[r0:r0 + 128, :]),
                 writes=[ctx.b_tok], stream="ti_ld")
            for kb in range(4):
                pi = kb % 2
                for q in range(4):
                    k = kb * 4 + q
                    P.op("pe", lambda k=k, q=q, pi=pi: nc.tensor.transpose(
                        out=PSb[pi][:, q * 128:(q + 1) * 128], in_=ctx.tok[:, k * 128:(k + 1) * 128], identity=ctx.cf[:, 0, :]),
                        reads=[ctx.b_tok, ctx.b_cf], writes=[bPSb[pi]], accum=(q > 0))
                P.op("act", lambda kb=kb, i=i, pi=pi: nc.scalar.copy(
                    out=S.x[:, kb * 4:(kb + 1) * 4, i * 128:(i + 1) * 128],
                    in_=PSb[pi][:].rearrange("p (q t) -> p q t", q=4)),
                    reads=[bPSb[pi]], writes=[S.b_x], accum=not (i == 0 and kb == 0))
        P.op("sp", lambda t=t: nc.sync.dma_start(
            out=xT.rearrange("(k p) t -> p k t", p=128)[:, :, t * 512:(t + 1) * 512], in_=S.x[:]),
            reads=[S.b_x], writes=[b_out], stream="ti_st", accum=True)


def emit_transpose_out(ctx, S, xT, out_tok, T, PSb, bPSb, b_in):
    nc, P = ctx.nc, ctx.P
    b_o = Buf("final")
    for t in range(T // 512):
        P.op("sp", lambda t=t: nc.sync.dma_start(
            out=S.x[:], in_=xT.rearrange("(k p) t -> p k t", p=128)[:, :, t * 512:(t + 1) * 512]),
            reads=[b_in], writes=[S.b_x], stream="to_ld")
        for i in range(4):
            for kb in range(4):
                pi = kb % 2
                for q in range(4):
                    k = kb * 4 + q
                    P.op("pe", lambda k=k, q=q, pi=pi, i=i: nc.tensor.transpose(
                        out=PSb[pi][:, q * 128:(q + 1) * 128], in_=S.x[:, k, i * 128:(i + 1) * 128], identity=ctx.cf[:, 0, :]),
                        reads=[S.b_x, ctx.b_cf], writes=[bPSb[pi]], accum=(q > 0))
                P.op("act", lambda kb=kb, pi=pi: nc.scalar.copy(out=ctx.tok[:, kb * 512:(kb + 1) * 512], in_=PSb[pi][:]),
                     reads=[bPSb[pi]], writes=[ctx.b_tok], accum=(kb > 0))
            r0 = t * 512 + i * 128
            P.op("sp", lambda r0=r0: nc.sync.dma_start(out=out_tok[r0:r0 + 128, :], in_=ctx.tok[:]),
                 reads=[ctx.b_tok], writes=[b_o], stream="to_st", accum=True)


def build_tok(T, n_ffn, tin, tout, cin):
    nc = bass.Bass("TRN2", target_bir_lowering=False)
    dt_ = nc.dram_tensor
    ctx = Ctx(); ctx.nc = nc; P = ctx.P = Prog(nc)
    P.same_engine_sync = False
    cst = dt_("cst", [128, 7, 128], F32, kind="ExternalInput").ap()
    if tin:
        x_in = dt_("x", [T, D], F32, kind="ExternalInput").ap()
    else:
        x_in = dt_("xT", [D, T], F32, kind="ExternalInput").ap()
    if tout:
        out = dt_("out", [T, D], F32, kind="ExternalOutput").ap()
    else:
        out = dt_("outT", [D, T], F32, kind="ExternalOutput").ap()
    mk_consts(ctx)
    load_consts(ctx, cst)
    S = ctx.ffn_sb = FfnSb(nc)
    ctx.tok = nc.alloc_sbuf_tensor("tok", [128, D], F32)
    ctx.b_tok = Buf("tok")
    b_w = Buf("w")
    ws = []
    if cin:
        yT = dt_("yT", [cin, T], BF16, kind="ExternalInput").ap()
        w_o = dt_("w_o", [cin, D], F32, kind="ExternalInput").ap()
        w_o_bf = dt_("w_o_bf", [cin, D], BF16, kind="Internal").ap()
        convert_weight(ctx, w_o, w_o_bf, 512)
    for f in range(n_ffn):
        g = dt_("g%d" % f, [D], F32, kind="ExternalInput").ap()
        wi = dt_("wi%d" % f, [D, 2 * DFF], F32, kind="ExternalInput").ap()
        wo = dt_("wo%d" % f, [DFF, D], F32, kind="ExternalInput").ap()
        wi_bf = dt_("wi_bf%d" % f, [D, 2 * DFF], BF16, kind="Internal").ap()
        wo_bf = dt_("wo_bf%d" % f, [DFF, D], BF16, kind="Internal").ap()
        convert_weight(ctx, wi, wi_bf, 128)
        convert_weight(ctx, wo, wo_bf, 256)
        ws.append((g, wi_bf, wo_bf))
    P.barrier()
    scr = [dt_("scr%d" % i, [D, T], F32, kind="Internal").ap() for i in range(2)]
    PSb = [S.ps_h[0][0], S.ps_h[0][1]]
    bPSb = [S.b_ps_h[0][0], S.b_ps_h[0][1]]
    cur, b_cur = x_in, Buf("xin")
    si = 0
    stages = []
    if tin:
        stages.append("tin")
    if cin:
        stages.append("proj")
    stages += ["ffn%d" % f for f in range(n_ffn)]
    for n_i, stg_ in enumerate(stages):
        last = (n_i == len(stages) - 1)
        dst = out if (last and not tout) else scr[si % 2]
        si += 1
        if stg_ == "tin":
            b_n = Buf("tin_out")
            emit_transpose_in(ctx, S, cur, dst, T, PSb, bPSb, b_n)
        elif stg_ == "proj":
            b_n = emit_ffn(ctx, cur, dst, T, None, None, w_o_bf, b_cur, b_w, "proj", pre_act=yT, CC=cin // 128, scale=1.0)
        else:
            g, wi_bf, wo_bf = ws[int(stg_[3:])]
            b_n = emit_ffn(ctx, cur, dst, T, g, wi_bf, wo_bf, b_cur, b_w, stg_)
        P.barrier()
        cur, b_cur = dst, b_n
    fin = ["ffn_st", "ti_st"]
    if tout:
        emit_transpose_out(ctx, S, cur, out, T, PSb, bPSb, b_cur)
        fin = ["to_st"]
    st = P.emit(final_wait_streams=fin)
    return nc, st


DIL_R = (1, 4, 16)
DIL_NC = 5120
KPAD = 1024


def t5_bucket_np(rel):
    n = np.abs(rel)
    far = 8 + (np.log(np.maximum(n, 1).astype(np.float32) / 8) / np.log(1024 / 8) * 8).astype(np.int32)
    far = np.minimum(far, 15)
    return np.where(rel > 0, 16, 0) + np.where(n < 8, n, far)


def host_bias_idx():
    a = np.arange(128)[:, None]
    c = np.arange(128)[None, :]
    idx = np.zeros((3, 2, 128, 128), np.int64)
    for gi, r in enumerate(DIL_R):
        for kt in range(2):
            idx[gi, kt] = t5_bucket_np((a - c - 64 + 128 * kt) * r)
    return idx


def host_dil_masks():
    a = np.arange(128)[:, None]
    c = np.arange(128)[None, :]
    m = np.zeros((128, 4, 128), np.float32)
    m[:, 0] = np.where(a >= c, 0.0, -30000.0)
    m[:, 1] = np.where(a <= c, 0.0, -30000.0)
    m[:, 2] = np.where((a >= c) & (a >= 64), 0.0, -30000.0)
    m[:, 3] = np.where((a <= c) & (a < 64), 0.0, -30000.0)
    return m


def build_dil(S):
    nc = bass.Bass("TRN2", target_bir_lowering=False)
    dt_ = nc.dram_tensor
    xT = dt_("xT", [D, S], F32, kind="ExternalInput").ap()
    gmix = dt_("gmix", [D], F32, kind="ExternalInput").ap()
    wc = dt_("wc", [D, DIL_NC], F32, kind="ExternalInput").ap()
    qkg = dt_("qkg", [128, 6], F32, kind="ExternalInput").ap()
    bias_t = dt_("bias_t", [128, 24, 128], F32, kind="ExternalInput").ap()
    masks = dt_("masks", [128, 4, 128], F32, kind="ExternalInput").ap()
    memT = dt_("memT", [D, 256], F32, kind="ExternalInput").ap()
    gmem = dt_("gmem", [D], F32, kind="ExternalInput").ap()
    wk = dt_("wk", [D, 512], F32, kind="ExternalInput").ap()
    wv = dt_("wv", [D, 512], F32, kind="ExternalInput").ap()
    kg = dt_("kg", [256], F32, kind="ExternalInput").ap()
    qg = dt_("qg", [256], F32, kind="ExternalInput").ap()
    cst = dt_("cst", [128, 7, 128], F32, kind="ExternalInput").ap()
    yT = dt_("yT", [1024, S], BF16, kind="ExternalOutput").ap()
    wc_bf = dt_("wc_bf", [D, DIL_NC], BF16, kind="Internal").ap()
    wk_bf = dt_("wk_bf", [D, 512], BF16, kind="Internal").ap()
    wv_bf = dt_("wv_bf", [D, 512], BF16, kind="Internal").ap()
    qn_d = [dt_("qn_d%d" % i, [512, S], BF16, kind="Internal").ap() for i in range(3)]
    kn_d = [dt_("kn_d%d" % i, [512, S], BF16, kind="Internal").ap() for i in range(3)]
    v_d = [dt_("v_d%d" % i, [S, 512], BF16, kind="Internal").ap() for i in range(3)]

    ctx = Ctx(); ctx.nc = nc; P = ctx.P = Prog(nc)
    mk_consts(ctx)
    load_consts(ctx, cst)
    convert_weight(ctx, wc, wc_bf, 256)
    convert_weight(ctx, wk, wk_bf, 512)
    convert_weight(ctx, wv, wv_bf, 512)
    ctx.b_wcv = Buf("wcv")
    P.barrier()
    PS = [nc.alloc_psum_tensor("ps%d" % i, [128, 512], F32) for i in range(8)]
    bPS = [Buf("ps%d" % i) for i in range(8)]
    N = NormSb(nc, 512, PS[0])
    N.b_ps_ss = bPS[0]
    M = MemSb(nc, 2)
    wbuf = [nc.alloc_sbuf_tensor("wbuf%d" % i, [128, KC, 512], BF16) for i in range(2)]
    b_wbuf = [Buf("wb0"), Buf("wb1")]
    stg = nc.alloc_sbuf_tensor("stg", [128, 4, 512], F32)
    b_stg = Buf("stg")
    stb = nc.alloc_sbuf_tensor("stb", [128, 4, 512], BF16)
    b_stb = Buf("stb")
    qkg_sb = nc.alloc_sbuf_tensor("qkg_sb", [128, 6], F32)
    b_par = Buf("par")
    P.op("sp", lambda: nc.sync.dma_start(out=qkg_sb[:], in_=qkg), writes=[b_par], stream="par")
    b_xin, b_q_d, b_k_d, b_v_d, b_out = Buf("xin"), Buf("q_d"), Buf("k_d"), Buf("v_d"), Buf("out")
    emit_mem_prep(ctx, M, N, memT, gmem, wk_bf, wv_bf, kg, qg, wbuf[0], b_wbuf[0], [PS[1], PS[2]], [bPS[1], bPS[2]])
    TT = 512
    yT_v = yT.rearrange("(j p) t -> p j t", p=128)
    wi = [0]

    def nextw():
        wi[0] += 1
        return wbuf[wi[0] % 2], b_wbuf[wi[0] % 2]

    for t in range(S // TT):
        t0 = t * TT
        emit_norm_tile(ctx, N, xT.rearrange("(k p) t -> p k t", p=128)[:, :, t0:t0 + TT], gmix, TT, b_xin)
        for gi in range(3):
            for qk in range(2):
                wb, bwb = nextw()

                def sink(jj, psap, bps, gi=gi, qk=qk):
                    P.op("act", lambda: nc.scalar.copy(out=stg[:, jj, :], in_=psap[:]), reads=[bps], writes=[b_stg], accum=(jj > 0))
                    P.op("act", lambda: nc.scalar.activation(out=M.sq[0][:], in_=psap[:], func=AF.Square),
                         reads=[bps], writes=[M.b_sq[0]])
                    P.op("pe", lambda: nc.tensor.matmul(PS[3][:], ctx.ones_bf[:], M.sq[0][:], start=True, stop=True),
                         reads=[M.b_sq[0], ctx.b_ones], writes=[bPS[3]])
                    P.op("act", lambda: nc.scalar.activation(out=M.rs[:], in_=PS[3][:], func=AF.Sqrt, bias=ctx.eps_t[:], scale=1.0 / 128),
                         reads=[bPS[3], ctx.b_eps], writes=[M.b_rs])
                    P.op("dve", lambda: nc.vector.reciprocal(out=M.rs[:], in_=M.rs[:]), reads=[M.b_rs], writes=[M.b_rs])
                    P.op("dve", lambda: nc.vector.scalar_tensor_tensor(
                        out=stb[:, jj, :], in0=stg[:, jj, :], scalar=qkg_sb[:, gi * 2 + qk:gi * 2 + qk + 1], in1=M.rs[:],
                        op0=ALU.mult, op1=ALU.mult), reads=[b_stg, b_par, M.b_rs], writes=[b_stb], accum=(jj > 0))
                emit_proj_fm(ctx, N, wb, bwb, wc_bf, (gi * 3 + qk) * 512, 512, [PS[1], PS[2]], [bPS[1], bPS[2]], TT, sink)
                dd_ = (qn_d if qk == 0 else kn_d)[gi]
                P.op("sp", lambda dd_=dd_, t0=t0: nc.sync.dma_start(
                    out=dd_.rearrange("(j p) t -> p j t", p=128)[:, :, t0:t0 + TT], in_=stb[:]),
                    reads=[b_stb], writes=[b_q_d if qk == 0 else b_k_d], stream="st_qk", accum=True)
            wb, bwb = nextw()

            def vsink(i, psap, bps):
                P.op("act", lambda: nc.scalar.copy(out=stb[:, i, :], in_=psap[:]), reads=[bps], writes=[b_stb], accum=(i > 0))
            emit_proj_tm(ctx, N, wb, bwb, wc_bf, (gi * 3 + 2) * 512, 512, [PS[1], PS[2]], [bPS[1], bPS[2]], TT, vsink)
            P.op("sp", lambda gi=gi, t0=t0: nc.sync.dma_start(
                out=v_d[gi].rearrange("(i p) f -> p i f", p=128)[:, t0 // 128:t0 // 128 + 4, :], in_=stb[:]),
                reads=[b_stb], writes=[b_v_d], stream="st_v", accum=True)
        wb, bwb = nextw()

        def qsink(jj, psap, bps):
            P.op("act", lambda: nc.scalar.copy(out=M.qT[:, jj, :], in_=psap[:]), reads=[bps], writes=[M.b_qT], accum=(jj > 0))
        emit_proj_fm(ctx, N, wb, bwb, wc_bf, 9 * 512, 512, [PS[1], PS[2]], [bPS[1], bPS[2]], TT, qsink)
        emit_mem_attn_tile(ctx, M, [PS[4], PS[5]], [bPS[4], bPS[5]],
                           lambda hh, dc, t0=t0: yT_v[:, 4 + hh * 2 + dc, t0:t0 + TT], b_out, "st_mo")
    P.barrier()

    sb = nc.alloc_sbuf_tensor
    BLK = 2048
    KT = [sb("KT%d" % i, [128, KPAD + S + KPAD], BF16) for i in range(3)]
    b_KT = [Buf("KT%d" % i) for i in range(3)]
    qb = [N.x[:, 2 * i:2 * i + 2, :].rearrange("p k t -> p (k t)").bitcast(BF16)[:, 0:BLK] for i in range(3)]
    b_qb = [Buf("qb%d" % i) for i in range(3)]
    onum = N.x[:, 8:12, :].rearrange("p k t -> p (k t)")
    oden = N.x[:, 12:16, :].rearrange("p k t -> p (k t)")
    b_acc = Buf("oacc")
    bm = [wbuf[0][:, 0:12, :].rearrange("p k t -> p (k t)").bitcast(F32).rearrange("p (n c) -> p n c", c=128),
          wbuf[1][:, 0:12, :].rearrange("p k t -> p (k t)").bitcast(F32).rearrange("p (n c) -> p n c", c=128)]
    msk = sb("msk", [128, 4, 128], F32)
    b_bm = Buf("bm")
    P.op("sp", lambda: nc.sync.dma_start(out=msk[:], in_=masks), writes=[b_bm], stream="bmld")
    P.op("sp", lambda: nc.sync.dma_start(out=bm[0], in_=bias_t), writes=[b_bm], stream="bmld", accum=True)
    for n in range(24):
        kt = n % 2
        P.op("dve", lambda n=n, kt=kt: nc.vector.tensor_tensor(out=bm[1][:, n, :], in0=bm[0][:, n, :], in1=msk[:, 2 + kt, :], op=ALU.add),
             reads=[b_bm], writes=[b_bm], accum=True)
    for n in range(24):
        kt = n % 2
        P.op("dve", lambda n=n, kt=kt: nc.vector.tensor_tensor(out=bm[0][:, n, :], in0=bm[0][:, n, :], in1=msk[:, kt, :], op=ALU.add),
             reads=[b_bm], writes=[b_bm], accum=True)
    for gi in range(3):
        P.op("pool", lambda gi=gi: nc.gpsimd.memset(KT[gi][:, 0:KPAD], 0.0), writes=[b_KT[gi]])
        P.op("pool", lambda gi=gi: nc.gpsimd.memset(KT[gi][:, KPAD + S:], 0.0), writes=[b_KT[gi]], accum=True)
    sbt = [sb("sbt%d" % i, [128, 128], F32) for i in range(2)]
    pT = [sb("dpT%d" % i, [128, 128], BF16) for i in range(2)]
    Vt = [sb("Vt%d" % i, [128, 128], BF16) for i in range(2)]
    ob = sb("ob", [128, BLK], BF16)
    rd = sb("rd", [128, BLK], F32)
    b_sbt, b_pT, b_Vt = [Buf("sbt0"), Buf("sbt1")], [Buf("dpT0"), Buf("dpT1")], [Buf("Vt0"), Buf("Vt1")]
    b_ob, b_rd = Buf("ob"), Buf("rd")
    ps_s, b_s = [PS[1], PS[2]], [bPS[1], bPS[2]]
    ps_o, b_o = PS[3], bPS[3]
    ps_d, b_d = PS[4], bPS[4]
    scale = 128 ** -0.5
    for h in range(4):
        for gi in range(3):
            P.op("sp", lambda gi=gi, h=h: nc.sync.dma_start(out=KT[gi][:, KPAD:KPAD + S], in_=kn_d[gi][h * 128:(h + 1) * 128, :]),
                 reads=[b_k_d], writes=[b_KT[gi]], stream="ld_K%d" % gi, accum=True)
        for blk in range(S // BLK):
            tb = blk * BLK
            P.op("pool", lambda: nc.gpsimd.memset(onum, 0.0), writes=[b_acc])
            P.op("pool", lambda: nc.gpsimd.memset(oden, 0.0), writes=[b_acc], accum=True)
            for gi, r in enumerate(DIL_R):
                L = S // r
                P.op("sp", lambda gi=gi, h=h, tb=tb: nc.sync.dma_start(out=qb[gi], in_=qn_d[gi][h * 128:(h + 1) * 128, tb:tb + BLK]),
                     reads=[b_q_d], writes=[b_qb[gi]], stream="ld_q%d" % gi)
                span = 128 * r
                for sbk in range(BLK // span):
                    for m in range(r):
                        j0 = (tb + span * sbk) // r
                        qsl = slice(span * sbk + m, span * sbk + m + 127 * r + 1, r)
                        for kt in range(2):
                            tok0 = tb + span * sbk + (128 * kt - 64) * r + m
                            first = (j0 == 0 and kt == 0)
                            lastt = (j0 + 128 == L and kt == 1)
                            n_b = (gi * 4 + h) * 2 + kt
                            bmt = bm[1][:, n_b, :] if (first or lastt) else bm[0][:, n_b, :]
                            ksl = slice(KPAD + tok0, KPAD + tok0 + 127 * r + 1, r)
                            P.op("pe", lambda gi=gi, ksl=ksl, qsl=qsl, kt=kt: nc.tensor.matmul(
                                ps_s[kt][:, 0:128], KT[gi][:, ksl], qb[gi][:, qsl], start=True, stop=True),
                                reads=[b_KT[gi], b_qb[gi]], writes=[b_s[kt]])
                            P.op("dve", lambda kt=kt, bmt=bmt: nc.vector.scalar_tensor_tensor(
                                out=sbt[kt][:], in0=ps_s[kt][:, 0:128], scalar=scale, in1=bmt, op0=ALU.mult, op1=ALU.add),
                                reads=[b_s[kt], b_bm], writes=[b_sbt[kt]])
                            P.op("act", lambda kt=kt: nc.scalar.activation(out=pT[kt][:], in_=sbt[kt][:], func=AF.Exp),
                                 reads=[b_sbt[kt]], writes=[b_pT[kt]])
                            a_lo, a_hi = (64, 128) if first else ((0, 64) if lastt else (0, 128))
                            if first or lastt:
                                P.op("pool", lambda kt=kt: nc.gpsimd.memset(Vt[kt][:], 0.0), writes=[b_Vt[kt]])
                            r_lo = tok0 + a_lo * r
                            P.op("sp", lambda gi=gi, kt=kt, r_lo=r_lo, a_lo=a_lo, a_hi=a_hi, r=r, h=h: nc.sync.dma_start(
                                out=Vt[kt][a_lo:a_hi, :],
                                in_=v_d[gi][r_lo:r_lo + (a_hi - a_lo - 1) * r + 1:r, h * 128:(h + 1) * 128]),
                                reads=[b_v_d], writes=[b_Vt[kt]], stream="ld_V%d" % kt, accum=(first or lastt))
                            P.op("pe", lambda kt=kt: nc.tensor.matmul(ps_o[:, 0:128], Vt[kt][:], pT[kt][:], start=(kt == 0), stop=(kt == 1)),
                                 reads=[b_Vt[kt], b_pT[kt]], writes=[b_o], accum=(kt > 0))
                            P.op("pe", lambda kt=kt: nc.tensor.matmul(ps_d[:, 0:128], ctx.ones_bf[:], pT[kt][:], start=(kt == 0), stop=(kt == 1)),
                                 reads=[ctx.b_ones, b_pT[kt]], writes=[b_d], accum=(kt > 0))
                        P.op("dve", lambda qsl=qsl: nc.vector.tensor_tensor(out=onum[:, qsl], in0=onum[:, qsl], in1=ps_o[:, 0:128], op=ALU.add),
                             reads=[b_o, b_acc], writes=[b_acc], accum=True)
                        P.op("dve", lambda qsl=qsl: nc.vector.tensor_tensor(out=oden[:, qsl], in0=oden[:, qsl], in1=ps_d[:, 0:128], op=ALU.add),
                             reads=[b_d, b_acc], writes=[b_acc], accum=True)
            P.op("dve", lambda: nc.vector.reciprocal(out=rd[:], in_=oden), reads=[b_acc], writes=[b_rd])
            P.op("dve", lambda: nc.vector.tensor_tensor(out=ob[:], in0=onum, in1=rd[:], op=ALU.mult),
                 reads=[b_acc, b_rd], writes=[b_ob])
            P.op("sp", lambda h=h, tb=tb: nc.sync.dma_start(out=yT[h * 128:(h + 1) * 128, tb:tb + BLK], in_=ob[:]),
                 reads=[b_ob], writes=[b_out], stream="st_o", accum=True)
    st = P.emit(final_wait_streams=["st_o", "st_mo0", "st_mo1"])
    return nc, st


def ssd_args(inp, i, hh, x1T, memT):
    j = i // 2
    w = inp["ssd_w_in"][j]
    ar = np.arange
    cols = np.concatenate([ar(hh * 1536, (hh + 1) * 1536), 3072 + ar(hh * 1536, (hh + 1) * 1536),
                           6144 + ar(hh * 512, (hh + 1) * 512), 7168 + ar(hh * 512, (hh + 1) * 512),
                           8192 + ar(hh * 24, (hh + 1) * 24), 8240 + ar(hh * 24, (hh + 1) * 24),
                           8288 + ar(hh * 512, (hh + 1) * 512)])
    cch = np.concatenate([ar(hh * 1536, (hh + 1) * 1536), 3072 + ar(hh * 512, (hh + 1) * 512),
                          4096 + ar(hh * 512, (hh + 1) * 512)])
    sl = slice(hh * 24, (hh + 1) * 24)
    c_ = np.ascontiguousarray
    return {
        "xT": x1T, "gmix": c_(inp["mix_norm"][i]), "wc": c_(w[:, cols]),
        "conv_w": c_(inp["ssd_conv_w"][j][:, cch].T), "conv_b": c_(inp["ssd_conv_b"][j][cch]),
        "dt_bias": c_(inp["ssd_dt_bias"][j][:, sl].reshape(1, 48)),
        "a_log": c_(inp["ssd_A_log"][j][:, sl].reshape(1, 48)),
        "dskip": c_(inp["ssd_D"][j][sl].reshape(1, 24)),
        "norm_g": c_(inp["ssd_norm"][j][hh * 1536:(hh + 1) * 1536].reshape(1, 1536)),
        "memT": memT, "gmem": c_(inp["mem_norm"][i]),
        "wk": c_(inp["mem_w_kv"][i][:, hh * 512:(hh + 1) * 512]),
        "wv": c_(inp["mem_w_kv"][i][:, 1024 + hh * 512:1024 + (hh + 1) * 512]),
        "kg": c_(inp["mem_k_gain"][i]), "qg": c_(inp["mem_q_gain"][i]), "cst": host_consts(),
    }


def dil_args(inp, i, hh, xT, memT):
    j = i // 2
    w = inp["dil_w_in"][j]
    ar = np.arange
    cols = []
    for gi in range(3):
        for which in range(3):
            cols.append(gi * 3072 + which * 1024 + ar(hh * 512, (hh + 1) * 512))
    cols.append(9216 + ar(hh * 512, (hh + 1) * 512))
    cols = np.concatenate(cols)
    c_ = np.ascontiguousarray
    qkg = np.stack([inp["dil_q_gain"][j][0], inp["dil_k_gain"][j][0], inp["dil_q_gain"][j][1], inp["dil_k_gain"][j][1],
                    inp["dil_q_gain"][j][2], inp["dil_k_gain"][j][2]], axis=1)
    idx = host_bias_idx()
    rb = inp["rel_bias"]
    bt = np.zeros((128, 24, 128), np.float32)
    for gi in range(3):
        for h in range(4):
            for kt in range(2):
                bt[:, (gi * 4 + h) * 2 + kt, :] = rb[idx[gi, kt], gi * 8 + hh * 4 + h]
    return {
        "xT": xT, "gmix": c_(inp["mix_norm"][i]), "wc": c_(w[:, cols]), "qkg": c_(qkg.astype(np.float32)),
        "bias_t": bt, "masks": host_dil_masks(),
        "memT": memT, "gmem": c_(inp["mem_norm"][i]),
        "wk": c_(inp["mem_w_kv"][i][:, hh * 512:(hh + 1) * 512]),
        "wv": c_(inp["mem_w_kv"][i][:, 1024 + hh * 512:1024 + (hh + 1) * 512]),
        "kg": c_(inp["mem_k_gain"][i]), "qg": c_(inp["mem_q_gain"][i]), "cst": host_consts(),
    }


_CACHE = {}


def _get(name, fn):
    if name not in _CACHE:
        _CACHE[name] = fn()[0]
    return _CACHE[name]


def kernel(**inputs):
    inp = {k: np.asarray(v) for k, v in inputs.items()}
    x = inp["x"]
    Bn, S, _ = x.shape
    T = S // 2
    NCORE = Bn * 2
    cores = list(range(NCORE))
    c_ = np.ascontiguousarray
    cst = host_consts()
    memT = [c_(inp["mem"][b].T) for b in range(Bn)]

    def ffn_w(f, i, k):
        return {"g%d" % f: c_(inp["ffn_norm"][i, k]), "wi%d" % f: c_(inp["ffn_w_in"][i, k]), "wo%d" % f: c_(inp["ffn_w_out"][i, k])}

    nc1 = _get("tok1", lambda: build_tok(T, 1, True, False, 0))
    maps = []
    for c in cores:
        b, hf = divmod(c, 2)
        m = {"x": c_(x[b, hf * T:(hf + 1) * T, :]), "cst": cst}
        m.update(ffn_w(0, 0, 0))
        maps.append(m)
    r1 = run_bass_kernel_spmd(nc1, maps, core_ids=cores).results
    x1T = [r["outT"] for r in r1]
    nc2 = _get("ssd", lambda: build_ssd(S))
    maps = []
    for c in cores:
        b, hh = divmod(c, 2)
        full = np.concatenate([x1T[2 * b], x1T[2 * b + 1]], axis=1)
        maps.append(ssd_args(inp, 0, hh, c_(full), memT[b]))
    r2 = run_bass_kernel_spmd(nc2, maps, core_ids=cores).results
    y2 = [r["yT"] for r in r2]
    del maps
    nc3 = _get("tok3", lambda: build_tok(T, 2, False, False, 4096))
    maps = []
    for c in cores:
        b, hf = divmod(c, 2)
        ts = slice(hf * T, (hf + 1) * T)
        ya, yb = y2[2 * b], y2[2 * b + 1]
        yin = np.concatenate([ya[0:1536, ts], yb[0:1536, ts], ya[1536:, ts], yb[1536:, ts]], axis=0)
        m = {"xT": x1T[c], "cst": cst, "yT": c_(yin), "w_o": c_(inp["ssd_w_out"][0])}
        m.update(ffn_w(0, 0, 1))
        m.update(ffn_w(1, 1, 0))
        maps.append(m)
    r3 = run_bass_kernel_spmd(nc3, maps, core_ids=cores).results
    x4T = [r["outT"] for r in r3]
    del maps, x1T, y2
    nc4 = _get("dil", lambda: build_dil(S))
    maps = []
    for c in cores:
        b, hh = divmod(c, 2)
        full = np.concatenate([x4T[2 * b], x4T[2 * b + 1]], axis=1)
        maps.append(dil_args(inp, 1, hh, c_(full), memT[b]))
    r4 = run_bass_kernel_spmd(nc4, maps, core_ids=cores).results
    y4 = [r["yT"] for r in r4]
    del maps
    nc5 = _get("tok5", lambda: build_tok(T, 1, False, True, 2048))
    maps = []
    for c in cores:
        b, hf = divmod(c, 2)
        ts = slice(hf * T, (hf + 1) * T)
        ya, yb = y4[2 * b], y4[2 * b + 1]
        yin = np.concatenate([ya[0:512, ts], yb[0:512, ts], ya[512:, ts], yb[512:, ts]], axis=0)
        m = {"xT": x4T[c], "cst": cst, "yT": c_(yin), "w_o": c_(inp["dil_w_out"][0])}
        m.update(ffn_w(0, 1, 1))
        maps.append(m)
    r5 = run_bass_kernel_spmd(nc5, maps, core_ids=cores).results
    out = np.zeros((Bn, S, D), np.float32)
    for c in cores:
        b, hf = divmod(c, 2)
        out[b, hf * T:(hf + 1) * T, :] = r5[c]["out"]
    return out
```
